# Optimizing a Trainium2 kernel written in Bass

```python
import jax, jax.numpy as jnp
from jax import lax
import numpy as np

D_MODEL = 1024
BATCH = 8
SEQ = 4096
DEPTH = 4

CHUNK = 64
N_PREV_CHUNKS = 8
BAND = N_PREV_CHUNKS + 1

D_MIX = D_MODEL
HEAD_DIM = 64
ATTN_WIDTH = D_MIX // 2
N_ATTN_HEADS = ATTN_WIDTH // HEAD_DIM
CONV_WIDTH = D_MIX // 4
CONV_K = 3
POOL_WIDTH = D_MIX - ATTN_WIDTH - CONV_WIDTH
POOL_WINDOWS = (2, 4, 8, 16)
N_POOL_GROUPS = len(POOL_WINDOWS)
POOL_GROUP = POOL_WIDTH // N_POOL_GROUPS
REL_CLIP = 128

D_IN = 3 * ATTN_WIDTH + 3 * CONV_WIDTH + POOL_WIDTH
D_FF = 4 * D_MODEL
EPS = 1e-6
NEG_INF = -1e30

kernel_name = "hybrid_chunked_attn_conv_pool_trunk"


def rms_norm(x, g):
    x32 = x.astype(jnp.float32)
    y = x32 * lax.rsqrt(jnp.mean(x32 * x32, axis=-1, keepdims=True) + EPS)
    return (y * g.astype(jnp.float32)).astype(x.dtype)


def chunked_band_attention(q, k, v, rel_bias):
    b, s, h, d = q.shape
    nc = s // CHUNK
    qc = q.reshape(b, nc, CHUNK, h, d)
    pad = ((0, 0), (N_PREV_CHUNKS, 0), (0, 0), (0, 0), (0, 0))
    kp = jnp.pad(k.reshape(b, nc, CHUNK, h, d), pad)
    vp = jnp.pad(v.reshape(b, nc, CHUNK, h, d), pad)
    band_idx = jnp.arange(nc)[:, None] + jnp.arange(BAND)[None, :]
    kb = kp[:, band_idx].reshape(b, nc, BAND * CHUNK, h, d)
    vb = vp[:, band_idx].reshape(b, nc, BAND * CHUNK, h, d)
    scores = jnp.einsum('bnqhd,bnkhd->bnhqk', qc, kb).astype(jnp.float32) * (d ** -0.5)
    qi = jnp.arange(CHUNK)[:, None]
    kj = jnp.arange(BAND * CHUNK)[None, :]
    dist = qi + N_PREV_CHUNKS * CHUNK - kj
    bias = rel_bias[:, jnp.clip(dist, -REL_CLIP, REL_CLIP) + REL_CLIP].astype(jnp.float32)
    valid = jnp.repeat(band_idx >= N_PREV_CHUNKS, CHUNK, axis=1)
    scores = jnp.where(valid[None, :, None, None, :], scores + bias[None, None], NEG_INF)
    p = jax.nn.softmax(scores, axis=-1).astype(v.dtype)
    out = jnp.einsum('bnhqk,bnkhd->bnqhd', p, vb)
    return out.reshape(b, s, h * d)


def gated_short_conv(gb, gc, hin, conv_w):
    z = gc * hin
    s = z.shape[1]
    zp = jnp.pad(z, ((0, 0), (CONV_K - 1, 0), (0, 0)))
    y = sum(conv_w[i] * zp[:, i:i + s] for i in range(CONV_K))
    return gb * y


def multiscale_pool(u, pool_w, pool_scale):
    b, s, c = u.shape
    u32 = u.astype(jnp.float32)
    cs = jnp.concatenate([jnp.zeros((b, 1, c), jnp.float32), jnp.cumsum(u32, axis=1)], axis=1)
    pos1 = jnp.arange(s) + 1
    outs = []
    for g, w in enumerate(POOL_WINDOWS):
        sl = slice(g * POOL_GROUP, (g + 1) * POOL_GROUP)
        csg = cs[:, :, sl]
        lag = jnp.pad(csg[:, :s + 1 - w], ((0, 0), (w - 1, 0), (0, 0)))
        cnt = jnp.minimum(pos1, w).astype(jnp.float32)[None, :, None]
        m = (csg[:, 1:] - lag) / cnt - u32[:, :, sl]
        outs.append(jnp.einsum('bsc,cd->bsd', m.astype(u.dtype), pool_w[g]))
    return jnp.concatenate(outs, axis=-1) * pool_scale


def setup_inputs(seed: int = 0) -> dict:
    key = jax.random.key(seed)
    ks = jax.random.split(key, 13)
    f32 = jnp.float32
    L = DEPTH
    x = jax.random.normal(ks[0], (BATCH, SEQ, D_MODEL), f32)
    norm1_g = 1.0 + 0.05 * jax.random.normal(ks[1], (L, D_MODEL), f32)
    w_in = jax.random.normal(ks[2], (L, D_MODEL, D_IN), f32) * D_MODEL ** -0.5
    q_norm_g = 1.0 + 0.05 * jax.random.normal(ks[3], (L, HEAD_DIM), f32)
    k_norm_g = 1.0 + 0.05 * jax.random.normal(ks[4], (L, HEAD_DIM), f32)
    rel_bias = 0.1 * jax.random.normal(ks[5], (L, N_ATTN_HEADS, 2 * REL_CLIP + 1), f32)
    conv_w = jax.random.normal(ks[6], (L, CONV_K, CONV_WIDTH), f32) * CONV_K ** -0.5
    pool_w = jax.random.normal(ks[7], (L, N_POOL_GROUPS, POOL_GROUP, POOL_GROUP), f32) * POOL_GROUP ** -0.5
    pool_scale = 0.5 + 0.1 * jax.random.normal(ks[8], (L, POOL_WIDTH), f32)
    w_out = jax.random.normal(ks[9], (L, D_MIX, D_MODEL), f32) * D_MIX ** -0.5
    norm2_g = 1.0 + 0.05 * jax.random.normal(ks[10], (L, D_MODEL), f32)
    w_mlp1 = jax.random.normal(ks[11], (L, D_MODEL, D_FF), f32) * D_MODEL ** -0.5
    w_mlp2 = jax.random.normal(ks[12], (L, D_FF, D_MODEL), f32) * D_FF ** -0.5
    return {"x": x, "norm1_g": norm1_g, "w_in": w_in, "q_norm_g": q_norm_g,
            "k_norm_g": k_norm_g, "rel_bias": rel_bias, "conv_w": conv_w,
            "pool_w": pool_w, "pool_scale": pool_scale, "w_out": w_out,
            "norm2_g": norm2_g, "w_mlp1": w_mlp1, "w_mlp2": w_mlp2}


def reference(x, norm1_g, w_in, q_norm_g, k_norm_g, rel_bias, conv_w, pool_w,
              pool_scale, w_out, norm2_g, w_mlp1, w_mlp2):
    b, s, _ = x.shape
    a = ATTN_WIDTH
    c = CONV_WIDTH
    for l in range(DEPTH):
        h = rms_norm(x, norm1_g[l])
        p = jnp.einsum('bsd,de->bse', h, w_in[l])
        q = p[..., 0:a].reshape(b, s, N_ATTN_HEADS, HEAD_DIM)
        k = p[..., a:2 * a].reshape(b, s, N_ATTN_HEADS, HEAD_DIM)
        v = p[..., 2 * a:3 * a].reshape(b, s, N_ATTN_HEADS, HEAD_DIM)
        o = 3 * a
        gb = p[..., o:o + c]
        gc = p[..., o + c:o + 2 * c]
        hin = p[..., o + 2 * c:o + 3 * c]
        u = p[..., o + 3 * c:]
        q = rms_norm(q, q_norm_g[l])
        k = rms_norm(k, k_norm_g[l])
        y_attn = chunked_band_attention(q, k, v, rel_bias[l])
        y_conv = gated_short_conv(gb, gc, hin, conv_w[l])
        y_pool = multiscale_pool(u, pool_w[l], pool_scale[l])
        mix = jnp.concatenate([y_attn, y_conv, y_pool], axis=-1)
        x = x + jnp.einsum('bse,ed->bsd', mix, w_out[l])
        h2 = rms_norm(x, norm2_g[l])
        f = jnp.square(jax.nn.relu(jnp.einsum('bsd,df->bsf', h2, w_mlp1[l])))
        x = x + jnp.einsum('bsf,fd->bsd', f, w_mlp2[l])
    return x
```

```python
import os
import numpy as np
from contextlib import ExitStack
import concourse.bass as bass
import concourse.mybir as mybir
from concourse.bass_utils import run_bass_kernel_spmd

F32 = mybir.dt.float32
BF16 = mybir.dt.bfloat16
AF = mybir.ActivationFunctionType
ALU = mybir.AluOpType

D = 1024
SEQ = 4096
NL = 4
TT = 1024
NSLOT = 23
RING = 3
ENGS = ["pe", "act", "dve", "pool", "sp"]
CP_L = 26
CP_INVW = NL * CP_L
CP_INVC = CP_INVW + 2
CP_CB = CP_INVC + 32
NCOL = CP_CB + NL * 8


class Op:
    __slots__ = ("eng", "fn", "deps", "dma", "signal", "seq", "waits")

    def __init__(self, eng, fn, deps, dma):
        self.eng = eng; self.fn = fn; self.deps = deps; self.dma = dma
        self.signal = False; self.seq = 0; self.waits = ()


class Prog:
    def __init__(self):
        self.ops = []
        self.recs = {}

    @staticmethod
    def _box(ap):
        t = ap.tensor
        row = 1
        for d in list(t.shape)[1:]:
            row *= int(d)
        e = 4 if t.dtype == F32 else 2
        off = int(ap.offset)
        p0 = off // row
        f0 = off % row
        dims = ap.ap
        pc = int(dims[0][1])
        ext = 0
        for st, cn in dims[1:]:
            ext += (int(cn) - 1) * int(st)
        return (t.name, p0, p0 + pc, f0 * e, (f0 + ext + 1) * e)

    def add(self, eng, fn, reads=(), writes=(), dma=None):
        oid = len(self.ops)
        ops = self.ops
        rb = [self._box(a) for a in reads]
        wb = [self._box(a) for a in writes]
        need = set()
        for (n, p0, p1, b0, b1) in rb:
            for r in self.recs.get(n, ()):
                if r[5] and r[0] < p1 and p0 < r[1] and r[2] < b1 and b0 < r[3]:
                    need.add((r[4], 0))
        for (n, p0, p1, b0, b1) in wb:
            for r in self.recs.get(n, ()):
                if r[0] < p1 and p0 < r[1] and r[2] < b1 and b0 < r[3]:
                    need.add((r[4], 1))
        deps = set()
        for pid, kind in need:
            Pp = ops[pid]
            if Pp.dma is None and Pp.eng == eng:
                if eng == "pe":
                    continue
                if dma is None and kind == 1:
                    continue
            deps.add(pid)
        ops.append(Op(eng, fn, deps, dma))
        for (n, p0, p1, b0, b1) in wb:
            lst = self.recs.setdefault(n, [])
            lst[:] = [r for r in lst if not (p0 <= r[0] and r[1] <= p1 and b0 <= r[2] and r[3] <= b1)]
            lst.append([p0, p1, b0, b1, oid, True])
        for (n, p0, p1, b0, b1) in rb:
            lst = self.recs.setdefault(n, [])
            for r in lst:
                if (not r[5]) and r[0] == p0 and r[1] == p1 and r[2] == b0 and r[3] == b1:
                    Pp = ops[r[4]]
                    if Pp.eng == eng and Pp.dma is None and dma is None:
                        r[4] = oid
                        break
            else:
                lst.append([p0, p1, b0, b1, oid, False])
        return oid

    def finalize(self):
        ops = self.ops
        for op in ops:
            for pid in op.deps:
                ops[pid].signal = True
        cnt = {e: 0 for e in ENGS}
        dcnt = {}
        for op in ops:
            if op.dma is not None:
                dcnt[op.dma] = dcnt.get(op.dma, 0) + 16
                op.seq = dcnt[op.dma]
            elif op.signal:
                cnt[op.eng] += 1
                op.seq = cnt[op.eng]
        waited = {e: {} for e in ENGS}
        for op in ops:
            w = {}
            for pid in op.deps:
                Pp = ops[pid]
                key = ("d", Pp.dma) if Pp.dma is not None else ("e", Pp.eng)
                if w.get(key, 0) < Pp.seq:
                    w[key] = Pp.seq
            wl = []
            we = waited[op.eng]
            for key, val in w.items():
                if we.get(key, 0) < val:
                    we[key] = val
                    wl.append((key, val))
            op.waits = wl
        return sorted(dcnt.keys())


def MM(out, lhsT, rhs, st, sp):
    return lambda e: e.matmul(out, lhsT=lhsT, rhs=rhs, start=st, stop=sp)


def ACT(out, in_, func, **kw):
    return lambda e: e.activation(out=out, in_=in_, func=func, **kw)


def TTO(out, in0, in1, op):
    return lambda e: e.tensor_tensor(out=out, in0=in0, in1=in1, op=op)


def TS(out, in0, s1, op0):
    return lambda e: e.tensor_scalar(out=out, in0=in0, scalar1=s1, scalar2=None, op0=op0)


def STT(out, in0, scalar, in1, op0, op1):
    return lambda e: e.scalar_tensor_tensor(out=out, in0=in0, scalar=scalar, in1=in1, op0=op0, op1=op1)


def CPY(out, in_):
    return lambda e: e.tensor_copy(out=out, in_=in_)


def MSET(ap, v):
    return lambda e: e.memset(ap, v)


def RCP(out, in_):
    return lambda e: e.reciprocal(out=out, in_=in_)


def DMA(out, in_):
    return lambda e: e.dma_start(out=out, in_=in_)


def build_nc(NT=4, NLAY=NL):
    nc = bass.Bass("TRN2", target_bir_lowering=False)
    S_ = NT * TT
    xT_d = nc.dram_tensor("xT", [8, 128, S_], F32, kind="ExternalInput").ap()
    wst_d = nc.dram_tensor("wst", [NL, NSLOT, 128, 4096], F32, kind="ExternalInput").ap()
    cpack_d = nc.dram_tensor("cpack", [128, NCOL], F32, kind="ExternalInput").ap()
    cmats_d = nc.dram_tensor("cmats", [128, 3, 128], F32, kind="ExternalInput").ap()
    pw_d = nc.dram_tensor("pw", [128, NL * 2, 128], F32, kind="ExternalInput").ap()
    rbt_d = nc.dram_tensor("rbt", [NL, 128, 2048], F32, kind="ExternalInput").ap()
    y_d = nc.dram_tensor("yT", [8, 128, S_], F32, kind="ExternalOutput").ap()

    P = Prog()
    es = ExitStack()

    def sb(name, shape, dt):
        return es.enter_context(nc.sbuf_tensor(name, shape, dt))

    x_sb = sb("x_sb", [128, 8, TT], F32)
    h_sb = sb("h_sb", [128, 8, TT], BF16)
    U = sb("U", [128, 32768], BF16)
    KH = sb("KH", [128, NL, 4, 512], BF16)
    VH = sb("VH", [128, NL, 2080], BF16)
    WR = sb("WR", [128, RING, 4096], BF16)
    EB = sb("EB", [128, NL, 8, 256], BF16)
    rstd = sb("rstd", [128, 2, 512], F32)
    sqb = sb("sqb", [128, 2, 512], BF16)
    rl = sb("rl", [128, 2, 512], F32)
    PT = sb("PT", [128, 2, 640], BF16)
    ao = sb("ao", [128, 2, 512], BF16)
    rc = sb("rc", [128, 2, 8], F32)
    cpk = sb("cpk", [128, NCOL], F32)
    cm = sb("cm", [128, 3, 128], BF16)
    PW = sb("PW", [128, NL * 2, 128], BF16)
    zh = sb("zh", [128, NL, 2, 2], F32)
    uh = sb("uh", [128, NL, 2, 16], F32)
    epsc = sb("epsc", [128, 1], F32)
    ps = es.enter_context(nc.psum_tensor("ps", [128, 8, 512], F32))

    q_v = U[:, 0:4096].rearrange("p (j t) -> p j t", j=4)
    k_v = U[:, 4096:8192].rearrange("p (j t) -> p j t", j=4)
    v_flat = U[:, 8192:8192 + 4160]
    v_v = v_flat.rearrange("p (b h d) -> p b h d", b=8, h=8)
    mix_v = U[:, 12352:12352 + 8192].rearrange("p (j t) -> p j t", j=8)
    xsq = U[:, 20544:20544 + 4096].rearrange("p (c t) -> p c t", c=8)
    gcz = U[:, 20544:20544 + 2056].bitcast(F32).rearrange("p (c t) -> p c t", c=2)
    gb_v = U[:, 22600:22600 + 2048].bitcast(F32).rearrange("p (c t) -> p c t", c=2)
    acc1 = U[:, 24648:24648 + 1024].bitcast(F32)
    mbf2 = U[:, 25672:25672 + 1024].rearrange("p (c t) -> p c t", c=2)
    ubuf = U[:, 26696:26696 + 2112].bitcast(F32).rearrange("p (c t) -> p c t", c=2)
    sA = U[:, 28808:28808 + 1056].bitcast(F32)
    sB = U[:, 29864:29864 + 1056].bitcast(F32)
    mbf_a = U[:, 30920:30920 + 1024].rearrange("p (c t) -> p c t", c=2)
    f_v = U[:, :].rearrange("p (s f t) -> p s f t", s=2, f=32)

    ones_m = cm[:, 0, :]
    blk_m = cm[:, 1, :]
    idn_m = cm[:, 2, :]
    eps_ap = epsc[:, 0:1]

    def col(i):
        return cpk[:, i:i + 1]

    MUL = ALU.mult
    ADD = ALU.add
    SUB = ALU.subtract

    st = {"big": 0, "aux": 0, "pool": 0, "rstd": 0, "sq": 0, "rl": 0, "use": 0, "issued": 0}
    n_uses = NT * NLAY * NSLOT

    def bigP():
        b = st["big"] % 4
        st["big"] += 1
        return ps[:, b, :]

    def auxP():
        b = 4 + st["aux"] % 2
        st["aux"] += 1
        return ps[:, b, :]

    def poolP():
        b = 6 + st["pool"] % 2
        st["pool"] += 1
        return ps[:, b, :]

    def rot(name, tensor):
        i = st[name] % 2
        st[name] += 1
        return tensor[:, i, :]

    def issue_w():
        u = st["issued"]
        if u >= n_uses:
            return
        st["issued"] += 1
        si = u % NSLOT
        l = (u // NSLOT) % NLAY
        slot = u % RING
        P.add("pool", DMA(WR[:, slot, :], wst_d[l, si]), writes=[WR[:, slot, :]], dma="w%d" % slot)

    def cur_w(k=0):
        return WR[:, (st["use"] + k) % RING, :]

    def release_w(n=1):
        for _ in range(n):
            st["use"] += 1
            issue_w()

    P.add("sp", DMA(cpk[:, :], cpack_d), writes=[cpk[:, :]], dma="c0")
    P.add("pool", DMA(cm[:, :, :], cmats_d), writes=[cm[:, :, :]], dma="c1")
    P.add("pool", DMA(PW[:, :, :], pw_d), writes=[PW[:, :, :]], dma="c2")
    for _ in range(RING):
        issue_w()
    P.add("dve", MSET(epsc[:, :], 1e-6), writes=[epsc[:, :]])
    P.add("dve", MSET(zh[:, :, :, :], 0.0), writes=[zh[:, :, :, :]])
    P.add("dve", MSET(uh[:, :, :, :], 0.0), writes=[uh[:, :, :, :]])
    for l in range(NLAY):
        stg = U[:, l * 4096:(l + 1) * 4096].bitcast(F32)
        stg3 = stg.rearrange("p (h e) -> p h e", h=8)
        P.add("sp", DMA(stg, rbt_d[l]), writes=[stg], dma="r%d" % l)
        cb = cpk[:, CP_CB + l * 8:CP_CB + l * 8 + 8]
        P.add("dve", TTO(stg3, stg3, cb.unsqueeze(2).to_broadcast([128, 8, 256]), SUB),
              reads=[stg, cb], writes=[stg])
        ebl = EB[:, l].rearrange("p h e -> p (h e)")
        P.add("act", ACT(ebl, stg, AF.Exp), reads=[stg], writes=[ebl])
        ebm = EB[64:128, l].rearrange("p h (j r) -> p h j r", j=2)[:, :, 1, 0:64]
        P.add("dve", MSET(ebm, 0.0), writes=[ebm])

    deferred = []

    def flush_deferred():
        while deferred:
            deferred.pop(0)()

    def phase_norm(gbase):
        for s in range(2):
            sl = slice(s * 512, (s + 1) * 512)
            for c in range(8):
                xs = x_sb[:, c, sl]
                if True:
                    P.add("act", ACT(xsq[:, c, :], xs, AF.Square), reads=[xs], writes=[xsq[:, c, :]])
                else:
                    P.add("dve", TTO(xsq[:, c, :], xs, xs, MUL), reads=[xs], writes=[xsq[:, c, :]])
            a = auxP()
            for c in range(8):
                P.add("pe", MM(a, ones_m, xsq[:, c, :], c == 0, c == 7), reads=[ones_m, xsq[:, c, :]], writes=[a])
            r = rot("rstd", rstd)
            P.add("act", ACT(r, a, AF.Sqrt, bias=eps_ap, scale=1.0), reads=[a, eps_ap], writes=[r])
            P.add("dve", RCP(r, r), reads=[r], writes=[r])
            for c in range(8):
                g = col(gbase + c)
                P.add("dve", STT(h_sb[:, c, sl], x_sb[:, c, sl], g, r, MUL, MUL),
                      reads=[x_sb[:, c, sl], g, r], writes=[h_sb[:, c, sl]])

    def mm_group(Pb, W, c0, rhs_t, sl):
        for c in range(8):
            P.add("pe", MM(Pb, W[:, c, c0:c0 + 128], rhs_t[:, c, sl], c == 0, c == 7),
                  reads=[W[:, c, c0:c0 + 128], rhs_t[:, c, sl]], writes=[Pb])

    def phase_qk(dst, gcol):
        W = cur_w().rearrange("p (c e) -> p c e", c=8)
        pend = []

        def post(j, s, Pb):
            sl = slice(s * 512, (s + 1) * 512)
            sq = rot("sq", sqb)
            P.add("act", ACT(sq, Pb, AF.Square), reads=[Pb], writes=[sq])
            a = auxP()
            P.add("pe", MM(a, blk_m, sq, True, True), reads=[blk_m, sq], writes=[a])
            r = rot("rstd", rstd)
            P.add("act", ACT(r, a, AF.Sqrt, bias=eps_ap, scale=1.0), reads=[a, eps_ap], writes=[r])
            P.add("dve", RCP(r, r), reads=[r], writes=[r])
            P.add("dve", STT(dst[:, j, sl], Pb, gcol, r, MUL, MUL), reads=[Pb, gcol, r], writes=[dst[:, j, sl]])

        for j in range(4):
            for s in range(2):
                Pb = bigP()
                mm_group(Pb, W, j * 128, h_sb, slice(s * 512, (s + 1) * 512))
                if pend:
                    post(*pend.pop())
                pend.append((j, s, Pb))
        post(*pend.pop())
        release_w()

    def phase_v():
        W = cur_w().rearrange("p (c e) -> p c e", c=8)
        onesc = v_v[:, :, :, 64:65]
        P.add("dve", MSET(onesc, 1.0), writes=[onesc])
        for b in range(8):
            Pb = bigP()
            for c in range(8):
                lh = h_sb[:, c, b * 128:(b + 1) * 128]
                P.add("pe", MM(Pb, lh, W[:, c, :], c == 0, c == 7), reads=[lh, W[:, c, :]], writes=[Pb])
            dst = v_v[:, b, :, 0:64]
            src = Pb.rearrange("p (h d) -> p h d", h=8)
            if b % 2 == 0:
                P.add("dve", CPY(dst, src), reads=[Pb], writes=[dst])
            else:
                P.add("act", ACT(dst, src, AF.Copy), reads=[Pb], writes=[dst])
        release_w()

    def phase_convpool(l, tile):
        W3 = cur_w(0).rearrange("p (c e) -> p c e", c=8)
        W4 = cur_w(1).rearrange("p (c e) -> p c e", c=8)
        cb0 = l * CP_L

        def cw(cc, i):
            return col(cb0 + 18 + cc * 3 + i)

        for s in range(2):
            sl = slice(s * 512, (s + 1) * 512)
            mbf = mbf_a if s == 0 else mbf2
            P.add("pool", CPY(gcz[:, :, 0:2], zh[:, l]), reads=[zh[:, l]], writes=[gcz[:, :, 0:2]])
            P.add("pool", CPY(ubuf[:, :, 0:16], uh[:, l]), reads=[uh[:, l]], writes=[ubuf[:, :, 0:16]])
            for cc in range(2):
                Pb = bigP()
                mm_group(Pb, W3, cc * 128, h_sb, sl)
                P.add("act", ACT(gb_v[:, cc, :], Pb, AF.Copy), reads=[Pb], writes=[gb_v[:, cc, :]])
            for cc in range(2):
                Pb = bigP()
                mm_group(Pb, W3, 256 + cc * 128, h_sb, sl)
                P.add("act", ACT(gcz[:, cc, 2:514], Pb, AF.Copy), reads=[Pb], writes=[gcz[:, cc, 2:514]])
            for cc in range(2):
                Pb = bigP()
                mm_group(Pb, W4, cc * 128, h_sb, sl)
                z = gcz[:, cc, 2:514]
                P.add("dve", TTO(z, Pb, z, MUL), reads=[Pb, z], writes=[z])
            for cc in range(2):
                Pb = bigP()
                mm_group(Pb, W4, 256 + cc * 128, h_sb, sl)
                P.add("act", ACT(ubuf[:, cc, 16:528], Pb, AF.Copy), reads=[Pb], writes=[ubuf[:, cc, 16:528]])
            P.add("pool", CPY(zh[:, l], gcz[:, :, 512:514]), reads=[gcz[:, :, 512:514]], writes=[zh[:, l]])
            P.add("pool", CPY(uh[:, l], ubuf[:, :, 512:528]), reads=[ubuf[:, :, 512:528]], writes=[uh[:, l]])
            for cc in range(2):
                P.add("act", ACT(acc1, gcz[:, cc, 2:514], AF.Copy, scale=cw(cc, 2)),
                      reads=[gcz[:, cc, 2:514], cw(cc, 2)], writes=[acc1])
                P.add("dve", STT(acc1, gcz[:, cc, 1:513], cw(cc, 1), acc1, MUL, ADD),
                      reads=[gcz[:, cc, 1:513], cw(cc, 1), acc1], writes=[acc1])
                P.add("dve", STT(acc1, gcz[:, cc, 0:512], cw(cc, 0), acc1, MUL, ADD),
                      reads=[gcz[:, cc, 0:512], cw(cc, 0), acc1], writes=[acc1])
                P.add("dve", TTO(mix_v[:, 4 + cc, sl], acc1, gb_v[:, cc, :], MUL),
                      reads=[acc1, gb_v[:, cc, :]], writes=[mix_v[:, 4 + cc, sl]])
            for cc in range(2):
                Uc = ubuf[:, cc, :]
                P.add("pool", TTO(sA[:, 1:528], Uc[:, 1:528], Uc[:, 0:527], ADD),
                      reads=[Uc], writes=[sA[:, 1:528]])
                P.add("pool", TTO(sB[:, 3:528], sA[:, 3:528], sA[:, 1:526], ADD),
                      reads=[sA[:, 1:528]], writes=[sB[:, 3:528]])
                if cc == 1:
                    P.add("pool", TTO(sA[:, 7:528], sB[:, 7:528], sB[:, 3:524], ADD),
                          reads=[sB[:, 3:528]], writes=[sA[:, 7:528]])
                    P.add("pool", TTO(sB[:, 15:528], sA[:, 15:528], sA[:, 7:520], ADD),
                          reads=[sA[:, 7:528]], writes=[sB[:, 15:528]])
                for (pr, src) in ((slice(0, 64), sA), (slice(64, 128), sB)):
                    iw = cpk[pr, CP_INVW + cc:CP_INVW + cc + 1]
                    P.add("dve", STT(mbf[pr, cc, :], src[pr, 16:528], iw, Uc[pr, 16:528], MUL, SUB),
                          reads=[src[pr, 16:528], iw, Uc[pr, 16:528]], writes=[mbf[pr, cc, :]])
                    if tile == 0 and s == 0:
                        ic = cpk[pr, CP_INVC + cc * 16:CP_INVC + cc * 16 + 16]
                        P.add("dve", TTO(src[pr, 16:32], src[pr, 16:32], ic, MUL),
                              reads=[src[pr, 16:32], ic], writes=[src[pr, 16:32]])
                        P.add("dve", TTO(mbf[pr, cc, 0:16], src[pr, 16:32], Uc[pr, 16:32], SUB),
                              reads=[src[pr, 16:32], Uc[pr, 16:32]], writes=[mbf[pr, cc, 0:16]])

            def pool_mm(l=l, sl=sl, s=s, mbf=mbf):
                for cc in range(2):
                    pb = poolP()
                    P.add("pe", MM(pb, PW[:, l * 2 + cc, :], mbf[:, cc, :], True, True),
                          reads=[PW[:, l * 2 + cc, :], mbf[:, cc, :]], writes=[pb])
                    sc = col(cb0 + 24 + cc)
                    P.add("act", ACT(mix_v[:, 6 + cc, sl], pb, AF.Copy, scale=sc),
                          reads=[pb, sc], writes=[mix_v[:, 6 + cc, sl]])
            deferred.append(pool_mm)
        release_w(2)

    def phase_attn(l, tile):
        def S(m, h):
            jjs = [jj for jj in range(5) if (m - 4 + jj >= 0 or tile > 0)]
            lo = jjs[0] * 128
            j, hh = divmod(h, 2)
            pr = slice(hh * 64, hh * 64 + 64)
            scb = ps[:, 4 + 2 * hh:6 + 2 * hh, :].rearrange("p a b -> p (a b)")
            qs = slice(m * 128, (m + 1) * 128)
            for jj in jjs:
                kt = m - 4 + jj
                if kt >= 0:
                    lh = k_v[pr, j, kt * 128:(kt + 1) * 128]
                else:
                    lh = KH[pr, l, j, (kt + 4) * 128:(kt + 5) * 128]
                o = scb[:, jj * 128:(jj + 1) * 128]
                P.add("pe", MM(o, lh, q_v[pr, j, qs], True, True), reads=[lh, q_v[pr, j, qs]], writes=[o])
            ptb = PT[:, hh, :]
            P.add("act", ACT(ptb[:, lo:640], scb[:, lo:640], AF.Exp, scale=0.125),
                  reads=[scb[:, lo:640]], writes=[ptb[:, lo:640]])
            P.add("dve", TTO(ptb[:, 384:640], ptb[:, 384:640], EB[:, l, h, :], MUL),
                  reads=[ptb[:, 384:640], EB[:, l, h, :]], writes=[ptb[:, 384:640]])
            if 0 in jjs:
                P.add("dve", MSET(PT[0:64, hh, 64:128], 0.0), writes=[PT[0:64, hh, 64:128]])

        def PV(m, h):
            jjs = [jj for jj in range(5) if (m - 4 + jj >= 0 or tile > 0)]
            hh = h % 2
            ob = ps[:, h // 4, (h % 4) * 65:(h % 4) * 65 + 65]
            for jj in jjs:
                kt = m - 4 + jj
                if kt >= 0:
                    rh = v_v[:, kt, h, :]
                else:
                    rh = VH[:, l, (kt + 4) * 520 + h * 65:(kt + 4) * 520 + h * 65 + 65]
                lh = PT[:, hh, jj * 128:(jj + 1) * 128]
                P.add("pe", MM(ob, lh, rh, jj == jjs[0], jj == 4), reads=[lh, rh], writes=[ob])

        S(0, 0)
        for m in range(8):
            ab = m % 2
            qs = slice(m * 128, (m + 1) * 128)
            for h in range(8):
                if h + 1 < 8:
                    S(m, h + 1)
                PV(m, h)
                if h == 0 and m == 3:
                    flush_deferred()
                if h % 4 == 3:
                    g = h // 4
                    ob3 = ps[:, g, 0:260].rearrange("p (h d) -> p h d", h=4)
                    rcv = rc[:, ab, g * 4:(g + 1) * 4]
                    P.add("dve", RCP(rcv, ob3[:, :, 64]), reads=[ob3[:, :, 64]], writes=[rcv])
                    aov = ao[:, ab, g * 256:(g + 1) * 256]
                    P.add("dve", TTO(aov.rearrange("p (h d) -> p h d", h=4), ob3[:, :, 0:64],
                                     rcv.unsqueeze(2).to_broadcast([128, 4, 64]), MUL),
                          reads=[ps[:, g, 0:260], rcv], writes=[aov])
            if m + 1 < 8:
                S(m + 1, 0)
            tb = ps[:, 2 + ab, :]
            for j in range(4):
                lh = ao[:, ab, j * 128:(j + 1) * 128]
                P.add("pe", MM(tb[:, j * 128:(j + 1) * 128], lh, idn_m, True, True),
                      reads=[lh, idn_m], writes=[tb[:, j * 128:(j + 1) * 128]])
            P.add("act", ACT(mix_v[:, 0:4, qs], tb.rearrange("p (j t) -> p j t", j=4), AF.Copy),
                  reads=[tb], writes=[mix_v[:, 0:4, qs]])
        P.add("pool", CPY(KH[:, l], k_v[:, :, 512:1024]), reads=[k_v[:, :, 512:1024]], writes=[KH[:, l]])
        P.add("pool", CPY(VH[:, l, :], U[:, 8192 + 2080:8192 + 4160]),
              reads=[U[:, 8192 + 2080:8192 + 4160]], writes=[VH[:, l, :]])

    def phase_B():
        for dh in range(2):
            W = cur_w().rearrange("p (c e) -> p c e", c=8)
            for dcl in range(4):
                dc = dh * 4 + dcl
                for s in range(2):
                    sl = slice(s * 512, (s + 1) * 512)
                    Pb = bigP()
                    mm_group(Pb, W, dcl * 128, mix_v, sl)
                    xs = x_sb[:, dc, sl]
                    P.add("dve", TTO(xs, Pb, xs, ADD), reads=[Pb, xs], writes=[xs])
            release_w()

    def phase_M1():
        for fg in range(8):
            W = cur_w().rearrange("p (c e) -> p c e", c=8)
            for j in range(4):
                fc = fg * 4 + j
                for s in range(2):
                    sl = slice(s * 512, (s + 1) * 512)
                    Pb = bigP()
                    mm_group(Pb, W, j * 128, h_sb, sl)
                    r = rot("rl", rl)
                    P.add("act", ACT(r, Pb, AF.Relu), reads=[Pb], writes=[r])
                    P.add("dve", TTO(f_v[:, s, fc, :], r, r, MUL), reads=[r], writes=[f_v[:, s, fc, :]])
            release_w()

    def phase_M2():
        for dc in range(8):
            W = cur_w().rearrange("p (f d) -> p f d", f=32)
            for s in range(2):
                sl = slice(s * 512, (s + 1) * 512)
                Pb = bigP()
                for fc in range(32):
                    P.add("pe", MM(Pb, W[:, fc, :], f_v[:, s, fc, :], fc == 0, fc == 31),
                          reads=[W[:, fc, :], f_v[:, s, fc, :]], writes=[Pb])
                xs = x_sb[:, dc, sl]
                P.add("dve", TTO(xs, Pb, xs, ADD), reads=[Pb, xs], writes=[xs])
            release_w()

    import os
    kstop = int(os.environ.get("KSTOP", "99"))
    for tile in range(NT):
        t0 = tile * TT
        src = xT_d[:, :, t0:t0 + TT].rearrange("c p t -> p c t")
        P.add("sp", DMA(x_sb[:, :, :], src), writes=[x_sb[:, :, :]], dma="xin")
        for l in range(NLAY):
            cb0 = l * CP_L
            if kstop >= 1: phase_norm(cb0 + 0)
            if kstop >= 2: phase_qk(q_v, col(cb0 + 16))
            if kstop >= 3: phase_qk(k_v, col(cb0 + 17))
            if kstop >= 4: phase_v()
            if kstop >= 5: phase_convpool(l, tile)
            if kstop >= 6: phase_attn(l, tile)
            flush_deferred()
            if kstop >= 7: phase_B()
            if kstop >= 8: phase_norm(cb0 + 8)
            if kstop >= 9: phase_M1()
            if kstop >= 10: phase_M2()
        dst = y_d[:, :, t0:t0 + TT].rearrange("c p t -> p c t")
        P.add("sp", DMA(dst, x_sb[:, :, :]), reads=[x_sb[:, :, :]], dma="xout")
    P.add("sp", None, writes=[x_sb[:, :, :]])

    dkeys = P.finalize()
    sems = {}
    for e in ENGS:
        sems[("e", e)] = es.enter_context(nc.semaphore("s_" + e))
    for k in dkeys:
        sems[("d", k)] = es.enter_context(nc.semaphore("d_" + k))
    by_eng = {e: [] for e in ENGS}
    for op in P.ops:
        by_eng[op.eng].append(op)

    def emitter(en):
        def f(e):
            mysem = sems[("e", en)]
            for op in by_eng[en]:
                for key, val in op.waits:
                    e.wait_ge(sems[key], val)
                if op.fn is None:
                    continue
                ins = op.fn(e)
                if op.dma is not None:
                    ins.then_inc(sems[("d", op.dma)], 16)
                elif op.signal:
                    ins.then_inc(mysem, 1)
        return f

    with nc.Block() as block:
        block.tensor(emitter("pe"))
        block.scalar(emitter("act"))
        block.vector(emitter("dve"))
        block.gpsimd(emitter("pool"))
        block.sync(emitter("sp"))
    es.close()
    return nc


def prep_weights(norm1_g, w_in, q_norm_g, k_norm_g, rel_bias, conv_w, pool_w, pool_scale, w_out, norm2_g,
                 w_mlp1, w_mlp2):
    wst = np.empty((NL, NSLOT, 128, 4096), np.float32)
    for l in range(NL):
        wi = np.asarray(w_in[l], np.float32).reshape(8, 128, 5, 512)
        wst[l, 0:5] = wi.transpose(2, 1, 0, 3).reshape(5, 128, 4096)
        wo = np.asarray(w_out[l], np.float32).reshape(8, 128, 2, 512)
        wst[l, 5:7] = wo.transpose(2, 1, 0, 3).reshape(2, 128, 4096)
        w1 = np.asarray(w_mlp1[l], np.float32).reshape(8, 128, 8, 512)
        wst[l, 7:15] = w1.transpose(2, 1, 0, 3).reshape(8, 128, 4096)
        w2 = np.asarray(w_mlp2[l], np.float32).reshape(32, 128, 8, 128)
        wst[l, 15:23] = w2.transpose(2, 1, 0, 3).reshape(8, 128, 4096)
    cpack = np.zeros((128, NCOL), np.float32)
    p = np.arange(128)
    for l in range(NL):
        b = l * CP_L
        cpack[:, b:b + 8] = np.asarray(norm1_g[l], np.float32).reshape(8, 128).T
        cpack[:, b + 8:b + 16] = np.asarray(norm2_g[l], np.float32).reshape(8, 128).T
        cpack[:, b + 16] = np.asarray(q_norm_g[l], np.float32)[p % 64]
        cpack[:, b + 17] = np.asarray(k_norm_g[l], np.float32)[p % 64]
        cw = np.asarray(conv_w[l], np.float32)
        for cc in range(2):
            for i in range(3):
                cpack[:, b + 18 + cc * 3 + i] = cw[i, cc * 128 + p]
            cpack[:, b + 24 + cc] = np.asarray(pool_scale[l], np.float32)[cc * 128 + p]
        cpack[:, CP_CB + l * 8:CP_CB + l * 8 + 8] = np.asarray(rel_bias[l], np.float32)[:, 256][None, :]
    wins = np.array([[2, 4], [8, 16]])
    for cc in range(2):
        wv = wins[cc][p // 64].astype(np.float32)
        cpack[:, CP_INVW + cc] = np.float32(1.0) / wv
        for t in range(16):
            cpack[:, CP_INVC + cc * 16 + t] = np.float32(1.0) / np.minimum(np.float32(t + 1), wv)
    cmats = np.zeros((128, 3, 128), np.float32)
    cmats[:, 0, :] = 1.0 / 1024.0
    cmats[0:64, 1, 0:64] = 1.0 / 64.0
    cmats[64:128, 1, 64:128] = 1.0 / 64.0
    cmats[:, 2, :] = np.eye(128, dtype=np.float32)
    pw = np.zeros((128, NL * 2, 128), np.float32)
    for l in range(NL):
        for cc in range(2):
            for half in range(2):
                g = 2 * cc + half
                pw[half * 64:(half + 1) * 64, l * 2 + cc, half * 64:(half + 1) * 64] = np.asarray(pool_w[l][g], np.float32)
    kk = np.arange(128)[:, None]
    r = np.arange(128)[None, :]
    idx3 = np.minimum(r - kk + 256, 256)
    idx4 = r - kk + 128
    rb = np.asarray(rel_bias, np.float32)
    rbt = np.empty((NL, 128, 8, 2, 128), np.float32)
    rbt[:, :, :, 0, :] = rb[:, :, idx3].transpose(0, 2, 1, 3)
    rbt[:, :, :, 1, :] = rb[:, :, idx4].transpose(0, 2, 1, 3)
    return wst, cpack, cmats, pw, np.ascontiguousarray(rbt.reshape(NL, 128, 2048))


_NC_CACHE = {}


def kernel(x, norm1_g, w_in, q_norm_g, k_norm_g, rel_bias, conv_w, pool_w, pool_scale, w_out, norm2_g,
           w_mlp1, w_mlp2):
    x = np.asarray(x, np.float32)
    B, S, _ = x.shape
    NT = S // TT
    wst, cpack, cmats, pw, rbt = prep_weights(norm1_g, w_in, q_norm_g, k_norm_g, rel_bias, conv_w, pool_w,
                                              pool_scale, w_out, norm2_g, w_mlp1, w_mlp2)
    key = (NT, NL)
    if key not in _NC_CACHE:
        _NC_CACHE[key] = build_nc(NT, NL)
    nc = _NC_CACHE[key]
    in_maps = []
    for b in range(B):
        xT = np.ascontiguousarray(x[b].T).reshape(8, 128, S)
        in_maps.append({"xT": xT, "wst": wst, "cpack": cpack, "cmats": cmats, "pw": pw, "rbt": rbt})
    res = run_bass_kernel_spmd(nc, in_maps, core_ids=list(range(B)))
    out = np.empty((B, S, D), np.float32)
    for b in range(B):
        out[b] = np.asarray(res.results[b]["yT"], np.float32).reshape(D, S).T
    return out
```

```python
import os
import numpy as np
from contextlib import ExitStack
import concourse.bass as bass
import concourse.mybir as mybir
from concourse.bass_utils import run_bass_kernel_spmd

F32 = mybir.dt.float32
BF16 = mybir.dt.bfloat16
AF = mybir.ActivationFunctionType
ALU = mybir.AluOpType

D = 1024
SEQ = 4096
NL = 4
TT = 1024
NSLOT = 23
RING = 3
ENGS = ["pe", "act", "dve", "pool", "sp"]
CP_L = 26
CP_INVW = NL * CP_L
CP_INVC = CP_INVW + 2
CP_CB = CP_INVC + 32
NCOL = CP_CB + NL * 8


class Op:
    __slots__ = ("eng", "fn", "deps", "dma", "signal", "seq", "waits")

    def __init__(self, eng, fn, deps, dma):
        self.eng = eng; self.fn = fn; self.deps = deps; self.dma = dma
        self.signal = False; self.seq = 0; self.waits = ()


class Prog:
    def __init__(self):
        self.ops = []
        self.recs = {}
        self.tags = [] if os.environ.get("KTAGS") else None
        self.tag = "pro"

    @staticmethod
    def _box(ap):
        t = ap.tensor
        row = 1
        for d in list(t.shape)[1:]:
            row *= int(d)
        e = 4 if t.dtype == F32 else 2
        off = int(ap.offset)
        p0 = off // row
        f0 = off % row
        dims = ap.ap
        pc = int(dims[0][1])
        ext = 0
        for st, cn in dims[1:]:
            ext += (int(cn) - 1) * int(st)
        return (t.name, p0, p0 + pc, f0 * e, (f0 + ext + 1) * e)

    def add(self, eng, fn, reads=(), writes=(), dma=None):
        oid = len(self.ops)
        ops = self.ops
        rb = [self._box(a) for a in reads]
        wb = [self._box(a) for a in writes]
        need = set()
        for (n, p0, p1, b0, b1) in rb:
            for r in self.recs.get(n, ()):
                if r[5] and r[0] < p1 and p0 < r[1] and r[2] < b1 and b0 < r[3]:
                    need.add((r[4], 0))
        for (n, p0, p1, b0, b1) in wb:
            for r in self.recs.get(n, ()):
                if r[0] < p1 and p0 < r[1] and r[2] < b1 and b0 < r[3]:
                    need.add((r[4], 1))
        deps = set()
        for pid, kind in need:
            Pp = ops[pid]
            if Pp.dma is None and Pp.eng == eng:
                if eng == "pe":
                    continue
                if dma is None and kind == 1:
                    continue
            deps.add(pid)
        ops.append(Op(eng, fn, deps, dma))
        if self.tags is not None:
            self.tags.append((eng, self.tag))
        for (n, p0, p1, b0, b1) in wb:
            lst = self.recs.setdefault(n, [])
            lst[:] = [r for r in lst if not (p0 <= r[0] and r[1] <= p1 and b0 <= r[2] and r[3] <= b1)]
            lst.append([p0, p1, b0, b1, oid, True])
        for (n, p0, p1, b0, b1) in rb:
            lst = self.recs.setdefault(n, [])
            for r in lst:
                if (not r[5]) and r[0] == p0 and r[1] == p1 and r[2] == b0 and r[3] == b1:
                    Pp = ops[r[4]]
                    if Pp.eng == eng and Pp.dma is None and dma is None:
                        r[4] = oid
                        break
            else:
                lst.append([p0, p1, b0, b1, oid, False])
        return oid

    def finalize(self):
        ops = self.ops
        for op in ops:
            for pid in op.deps:
                ops[pid].signal = True
        cnt = {e: 0 for e in ENGS}
        dcnt = {}
        for op in ops:
            if op.dma is not None:
                dcnt[op.dma] = dcnt.get(op.dma, 0) + 16
                op.seq = dcnt[op.dma]
            elif op.signal:
                cnt[op.eng] += 1
                op.seq = cnt[op.eng]
        waited = {e: {} for e in ENGS}
        for op in ops:
            w = {}
            for pid in op.deps:
                Pp = ops[pid]
                key = ("d", Pp.dma) if Pp.dma is not None else ("e", Pp.eng)
                if w.get(key, 0) < Pp.seq:
                    w[key] = Pp.seq
            wl = []
            we = waited[op.eng]
            for key, val in w.items():
                if we.get(key, 0) < val:
                    we[key] = val
                    wl.append((key, val))
            op.waits = wl
        return sorted(dcnt.keys())


def MM(out, lhsT, rhs, st, sp):
    return lambda e: e.matmul(out, lhsT=lhsT, rhs=rhs, start=st, stop=sp)


def ACT(out, in_, func, **kw):
    return lambda e: e.activation(out=out, in_=in_, func=func, **kw)


def TTO(out, in0, in1, op):
    return lambda e: e.tensor_tensor(out=out, in0=in0, in1=in1, op=op)


def TS(out, in0, s1, op0):
    return lambda e: e.tensor_scalar(out=out, in0=in0, scalar1=s1, scalar2=None, op0=op0)


def STT(out, in0, scalar, in1, op0, op1):
    return lambda e: e.scalar_tensor_tensor(out=out, in0=in0, scalar=scalar, in1=in1, op0=op0, op1=op1)


def CPY(out, in_):
    return lambda e: e.tensor_copy(out=out, in_=in_)


def MSET(ap, v):
    return lambda e: e.memset(ap, v)


def RCP(out, in_):
    return lambda e: e.reciprocal(out=out, in_=in_)


def DMA(out, in_):
    return lambda e: e.dma_start(out=out, in_=in_)


def build_nc(NT=4, NLAY=NL):
    nc = bass.Bass("TRN2", target_bir_lowering=False)
    S_ = NT * TT
    xT_d = nc.dram_tensor("xT", [8, 128, S_], F32, kind="ExternalInput").ap()
    wst_d = nc.dram_tensor("wst", [NL, NSLOT, 128, 4096], F32, kind="ExternalInput").ap()
    cpack_d = nc.dram_tensor("cpack", [128, NCOL], F32, kind="ExternalInput").ap()
    cmats_d = nc.dram_tensor("cmats", [128, 3, 128], F32, kind="ExternalInput").ap()
    pw_d = nc.dram_tensor("pw", [128, NL * 2, 128], F32, kind="ExternalInput").ap()
    rbt_d = nc.dram_tensor("rbt", [NL, 128, 2048], F32, kind="ExternalInput").ap()
    y_d = nc.dram_tensor("yT", [8, 128, S_], F32, kind="ExternalOutput").ap()

    P = Prog()
    es = ExitStack()

    def sb(name, shape, dt):
        return es.enter_context(nc.sbuf_tensor(name, shape, dt))

    x_sb = sb("x_sb", [128, 8, TT], F32)
    h_sb = sb("h_sb", [128, 8, TT], BF16)
    U = sb("U", [128, 32768], BF16)
    KH = sb("KH", [128, NL, 4, 512], BF16)
    VH = sb("VH", [128, NL, 2080], BF16)
    WR = sb("WR", [128, RING, 4096], BF16)
    EB = sb("EB", [128, NL, 8, 256], BF16)
    rstd = sb("rstd", [128, 2, 512], F32)
    sqb = sb("sqb", [128, 2, 512], BF16)
    rl = sb("rl", [128, 2, 512], F32)
    PT = sb("PT", [128, 3, 640], BF16)
    ao = sb("ao", [128, 2, 512], BF16)
    rc = sb("rc", [128, 2, 8], F32)
    cpk = sb("cpk", [128, NCOL], F32)
    cm = sb("cm", [128, 3, 128], BF16)
    PW = sb("PW", [128, NL * 2, 128], BF16)
    zh = sb("zh", [128, NL, 2, 2], F32)
    uh = sb("uh", [128, NL, 2, 16], F32)
    epsc = sb("epsc", [128, 1], F32)
    ps = es.enter_context(nc.psum_tensor("ps", [128, 8, 512], F32))

    q_v = U[:, 0:4096].rearrange("p (j t) -> p j t", j=4)
    k_v = U[:, 4096:8192].rearrange("p (j t) -> p j t", j=4)
    v_flat = U[:, 8192:8192 + 4160]
    v_v = v_flat.rearrange("p (b h d) -> p b h d", b=8, h=8)
    mix_v = U[:, 12352:12352 + 8192].rearrange("p (j t) -> p j t", j=8)
    xsq = U[:, 20544:20544 + 4096].rearrange("p (c t) -> p c t", c=8)
    gcz = U[:, 20544:20544 + 2056].bitcast(F32).rearrange("p (c t) -> p c t", c=2)
    gb_v = U[:, 22600:22600 + 2048].bitcast(F32).rearrange("p (c t) -> p c t", c=2)
    acc1 = U[:, 24648:24648 + 1024].bitcast(F32)
    mbf2 = U[:, 25672:25672 + 1024].rearrange("p (c t) -> p c t", c=2)
    ubuf = U[:, 26696:26696 + 2112].bitcast(F32).rearrange("p (c t) -> p c t", c=2)
    sA = U[:, 28808:28808 + 1056].bitcast(F32)
    sB = U[:, 29864:29864 + 1056].bitcast(F32)
    mbf_a = U[:, 30920:30920 + 1024].rearrange("p (c t) -> p c t", c=2)
    f_v = U[:, :].rearrange("p (s f t) -> p s f t", s=2, f=32)

    ones_m = cm[:, 0, :]
    blk_m = cm[:, 1, :]
    idn_m = cm[:, 2, :]
    eps_ap = epsc[:, 0:1]

    def col(i):
        return cpk[:, i:i + 1]

    MUL = ALU.mult
    ADD = ALU.add
    SUB = ALU.subtract

    st = {"big": 0, "aux": 0, "pool": 0, "rstd": 0, "sq": 0, "rl": 0, "use": 0, "issued": 0}
    n_uses = NT * NLAY * NSLOT

    def bigP():
        b = st["big"] % 4
        st["big"] += 1
        return ps[:, b, :]

    def auxP():
        b = 4 + st["aux"] % 2
        st["aux"] += 1
        return ps[:, b, :]

    def poolP():
        b = 6 + st["pool"] % 2
        st["pool"] += 1
        return ps[:, b, :]

    def rot(name, tensor):
        i = st[name] % 2
        st[name] += 1
        return tensor[:, i, :]

    def issue_w():
        u = st["issued"]
        if u >= n_uses:
            return
        st["issued"] += 1
        si = u % NSLOT
        l = (u // NSLOT) % NLAY
        slot = u % RING
        P.add("pool", DMA(WR[:, slot, :], wst_d[l, si]), writes=[WR[:, slot, :]], dma="w%d" % slot)

    def cur_w(k=0):
        return WR[:, (st["use"] + k) % RING, :]

    def release_w(n=1):
        for _ in range(n):
            st["use"] += 1
            issue_w()

    P.add("sp", DMA(cpk[:, :], cpack_d), writes=[cpk[:, :]], dma="c0")
    P.add("pool", DMA(cm[:, :, :], cmats_d), writes=[cm[:, :, :]], dma="c1")
    P.add("pool", DMA(PW[:, :, :], pw_d), writes=[PW[:, :, :]], dma="c2")
    for _ in range(RING):
        issue_w()
    P.add("dve", MSET(epsc[:, :], 1e-6), writes=[epsc[:, :]])
    P.add("dve", MSET(zh[:, :, :, :], 0.0), writes=[zh[:, :, :, :]])
    P.add("dve", MSET(uh[:, :, :, :], 0.0), writes=[uh[:, :, :, :]])
    for l in range(NLAY):
        stg = U[:, l * 4096:(l + 1) * 4096].bitcast(F32)
        stg3 = stg.rearrange("p (h e) -> p h e", h=8)
        P.add("sp", DMA(stg, rbt_d[l]), writes=[stg], dma="r%d" % l)
        cb = cpk[:, CP_CB + l * 8:CP_CB + l * 8 + 8]
        P.add("dve", TTO(stg3, stg3, cb.unsqueeze(2).to_broadcast([128, 8, 256]), SUB),
              reads=[stg, cb], writes=[stg])
        ebl = EB[:, l].rearrange("p h e -> p (h e)")
        P.add("act", ACT(ebl, stg, AF.Exp), reads=[stg], writes=[ebl])
        ebm = EB[64:128, l].rearrange("p h (j r) -> p h j r", j=2)[:, :, 1, 0:64]
        P.add("dve", MSET(ebm, 0.0), writes=[ebm])

    deferred = []

    def flush_deferred():
        while deferred:
            deferred.pop(0)()

    def phase_norm(gbase):
        for s in range(2):
            sl = slice(s * 512, (s + 1) * 512)
            for c in range(8):
                xs = x_sb[:, c, sl]
                if True:
                    P.add("act", ACT(xsq[:, c, :], xs, AF.Square), reads=[xs], writes=[xsq[:, c, :]])
                else:
                    P.add("dve", TTO(xsq[:, c, :], xs, xs, MUL), reads=[xs], writes=[xsq[:, c, :]])
            a = auxP()
            for c in range(8):
                P.add("pe", MM(a, ones_m, xsq[:, c, :], c == 0, c == 7), reads=[ones_m, xsq[:, c, :]], writes=[a])
            r = rot("rstd", rstd)
            P.add("act", ACT(r, a, AF.Ln, bias=eps_ap, scale=1.0), reads=[a, eps_ap], writes=[r])
            P.add("act", ACT(r, r, AF.Exp, scale=-0.5), reads=[r], writes=[r])
            for c in range(8):
                g = col(gbase + c)
                P.add("dve", STT(h_sb[:, c, sl], x_sb[:, c, sl], g, r, MUL, MUL),
                      reads=[x_sb[:, c, sl], g, r], writes=[h_sb[:, c, sl]])

    def mm_group(Pb, W, c0, rhs_t, sl):
        for c in range(8):
            P.add("pe", MM(Pb, W[:, c, c0:c0 + 128], rhs_t[:, c, sl], c == 0, c == 7),
                  reads=[W[:, c, c0:c0 + 128], rhs_t[:, c, sl]], writes=[Pb])

    def phase_qk(dst, gcol):
        W = cur_w().rearrange("p (c e) -> p c e", c=8)
        pend = []

        def post(j, s, Pb):
            sl = slice(s * 512, (s + 1) * 512)
            sq = rot("sq", sqb)
            P.add("act", ACT(sq, Pb, AF.Square), reads=[Pb], writes=[sq])
            a = auxP()
            P.add("pe", MM(a, blk_m, sq, True, True), reads=[blk_m, sq], writes=[a])
            r = rot("rstd", rstd)
            P.add("act", ACT(r, a, AF.Ln, bias=eps_ap, scale=1.0), reads=[a, eps_ap], writes=[r])
            P.add("act", ACT(r, r, AF.Exp, scale=-0.5), reads=[r], writes=[r])
            P.add("dve", STT(dst[:, j, sl], Pb, gcol, r, MUL, MUL), reads=[Pb, gcol, r], writes=[dst[:, j, sl]])

        for s in range(2):
            for j in range(4):
                Pb = bigP()
                mm_group(Pb, W, j * 128, h_sb, slice(s * 512, (s + 1) * 512))
                if pend:
                    post(*pend.pop())
                pend.append((j, s, Pb))
        post(*pend.pop())
        release_w()

    def phase_v():
        W = cur_w().rearrange("p (c e) -> p c e", c=8)
        onesc = v_v[:, :, :, 64:65]
        P.add("dve", MSET(onesc, 1.0), writes=[onesc])
        for b in range(8):
            Pb = bigP()
            for c in range(8):
                lh = h_sb[:, c, b * 128:(b + 1) * 128]
                P.add("pe", MM(Pb, lh, W[:, c, :], c == 0, c == 7), reads=[lh, W[:, c, :]], writes=[Pb])
            dst = v_v[:, b, :, 0:64]
            src = Pb.rearrange("p (h d) -> p h d", h=8)
            if b % 2 == 0:
                P.add("dve", CPY(dst, src), reads=[Pb], writes=[dst])
            else:
                P.add("act", ACT(dst, src, AF.Copy), reads=[Pb], writes=[dst])
        release_w()

    def phase_convpool(l, tile):
        W3 = cur_w(0).rearrange("p (c e) -> p c e", c=8)
        W4 = cur_w(1).rearrange("p (c e) -> p c e", c=8)
        cb0 = l * CP_L

        def cw(cc, i):
            return col(cb0 + 18 + cc * 3 + i)

        for s in range(2):
            sl = slice(s * 512, (s + 1) * 512)
            mbf = mbf_a if s == 0 else mbf2
            P.add("pool", CPY(gcz[:, :, 0:2], zh[:, l]), reads=[zh[:, l]], writes=[gcz[:, :, 0:2]])
            P.add("pool", CPY(ubuf[:, :, 0:16], uh[:, l]), reads=[uh[:, l]], writes=[ubuf[:, :, 0:16]])
            for cc in range(2):
                Pb = bigP()
                mm_group(Pb, W3, cc * 128, h_sb, sl)
                P.add("act", ACT(gb_v[:, cc, :], Pb, AF.Copy), reads=[Pb], writes=[gb_v[:, cc, :]])
            for cc in range(2):
                Pb = bigP()
                mm_group(Pb, W3, 256 + cc * 128, h_sb, sl)
                P.add("act", ACT(gcz[:, cc, 2:514], Pb, AF.Copy), reads=[Pb], writes=[gcz[:, cc, 2:514]])
            for cc in range(2):
                Pb = bigP()
                mm_group(Pb, W4, cc * 128, h_sb, sl)
                z = gcz[:, cc, 2:514]
                P.add("dve", TTO(z, Pb, z, MUL), reads=[Pb, z], writes=[z])
            for cc in range(2):
                Pb = bigP()
                mm_group(Pb, W4, 256 + cc * 128, h_sb, sl)
                P.add("act", ACT(ubuf[:, cc, 16:528], Pb, AF.Copy), reads=[Pb], writes=[ubuf[:, cc, 16:528]])
            P.add("pool", CPY(zh[:, l], gcz[:, :, 512:514]), reads=[gcz[:, :, 512:514]], writes=[zh[:, l]])
            P.add("pool", CPY(uh[:, l], ubuf[:, :, 512:528]), reads=[ubuf[:, :, 512:528]], writes=[uh[:, l]])
            for cc in range(2):
                P.add("act", ACT(acc1, gcz[:, cc, 2:514], AF.Copy, scale=cw(cc, 2)),
                      reads=[gcz[:, cc, 2:514], cw(cc, 2)], writes=[acc1])
                P.add("dve", STT(acc1, gcz[:, cc, 1:513], cw(cc, 1), acc1, MUL, ADD),
                      reads=[gcz[:, cc, 1:513], cw(cc, 1), acc1], writes=[acc1])
                P.add("dve", STT(acc1, gcz[:, cc, 0:512], cw(cc, 0), acc1, MUL, ADD),
                      reads=[gcz[:, cc, 0:512], cw(cc, 0), acc1], writes=[acc1])
                P.add("dve", TTO(mix_v[:, 4 + cc, sl], acc1, gb_v[:, cc, :], MUL),
                      reads=[acc1, gb_v[:, cc, :]], writes=[mix_v[:, 4 + cc, sl]])
            for cc in range(2):
                Uc = ubuf[:, cc, :]
                P.add("pool", TTO(sA[:, 1:528], Uc[:, 1:528], Uc[:, 0:527], ADD),
                      reads=[Uc], writes=[sA[:, 1:528]])
                P.add("pool", TTO(sB[:, 3:528], sA[:, 3:528], sA[:, 1:526], ADD),
                      reads=[sA[:, 1:528]], writes=[sB[:, 3:528]])
                if cc == 1:
                    P.add("pool", TTO(sA[:, 7:528], sB[:, 7:528], sB[:, 3:524], ADD),
                          reads=[sB[:, 3:528]], writes=[sA[:, 7:528]])
                    P.add("pool", TTO(sB[:, 15:528], sA[:, 15:528], sA[:, 7:520], ADD),
                          reads=[sA[:, 7:528]], writes=[sB[:, 15:528]])
                for (pr, src) in ((slice(0, 64), sA), (slice(64, 128), sB)):
                    iw = cpk[pr, CP_INVW + cc:CP_INVW + cc + 1]
                    P.add("dve", STT(mbf[pr, cc, :], src[pr, 16:528], iw, Uc[pr, 16:528], MUL, SUB),
                          reads=[src[pr, 16:528], iw, Uc[pr, 16:528]], writes=[mbf[pr, cc, :]])
                    if tile == 0 and s == 0:
                        ic = cpk[pr, CP_INVC + cc * 16:CP_INVC + cc * 16 + 16]
                        P.add("dve", TTO(src[pr, 16:32], src[pr, 16:32], ic, MUL),
                              reads=[src[pr, 16:32], ic], writes=[src[pr, 16:32]])
                        P.add("dve", TTO(mbf[pr, cc, 0:16], src[pr, 16:32], Uc[pr, 16:32], SUB),
                              reads=[src[pr, 16:32], Uc[pr, 16:32]], writes=[mbf[pr, cc, 0:16]])

            def pool_mm(l=l, sl=sl, s=s, mbf=mbf):
                for cc in range(2):
                    pb = poolP()
                    P.add("pe", MM(pb, PW[:, l * 2 + cc, :], mbf[:, cc, :], True, True),
                          reads=[PW[:, l * 2 + cc, :], mbf[:, cc, :]], writes=[pb])
                    sc = col(cb0 + 24 + cc)
                    P.add("act", ACT(mix_v[:, 6 + cc, sl], pb, AF.Copy, scale=sc),
                          reads=[pb, sc], writes=[mix_v[:, 6 + cc, sl]])
            deferred.append(pool_mm)
        release_w(2)

    def phase_attn(l, tile):
        def S(m, h):
            jjs = [jj for jj in range(5) if (m - 4 + jj >= 0 or tile > 0)]
            lo = jjs[0] * 128
            j, hh = divmod(h, 2)
            pr = slice(hh * 64, hh * 64 + 64)
            scb = ps[:, 4 + 2 * hh:6 + 2 * hh, :].rearrange("p a b -> p (a b)")
            qs = slice(m * 128, (m + 1) * 128)
            for jj in jjs:
                kt = m - 4 + jj
                if kt >= 0:
                    lh = k_v[pr, j, kt * 128:(kt + 1) * 128]
                else:
                    lh = KH[pr, l, j, (kt + 4) * 128:(kt + 5) * 128]
                o = scb[:, jj * 128:(jj + 1) * 128]
                P.add("pe", MM(o, lh, q_v[pr, j, qs], True, True), reads=[lh, q_v[pr, j, qs]], writes=[o])
            pti = (m * 8 + h) % 3
            ptb = PT[:, pti, :]
            P.add("act", ACT(ptb[:, lo:640], scb[:, lo:640], AF.Exp, scale=0.125),
                  reads=[scb[:, lo:640]], writes=[ptb[:, lo:640]])
            P.add("dve", TTO(ptb[:, 384:640], ptb[:, 384:640], EB[:, l, h, :], MUL),
                  reads=[ptb[:, 384:640], EB[:, l, h, :]], writes=[ptb[:, 384:640]])
            if 0 in jjs:
                P.add("dve", MSET(PT[0:64, pti, 64:128], 0.0), writes=[PT[0:64, pti, 64:128]])

        def PV(m, h):
            jjs = [jj for jj in range(5) if (m - 4 + jj >= 0 or tile > 0)]
            hh = h % 2
            ob = ps[:, h // 4, (h % 4) * 65:(h % 4) * 65 + 65]
            for jj in jjs:
                kt = m - 4 + jj
                if kt >= 0:
                    rh = v_v[:, kt, h, :]
                else:
                    rh = VH[:, l, (kt + 4) * 520 + h * 65:(kt + 4) * 520 + h * 65 + 65]
                lh = PT[:, (m * 8 + h) % 3, jj * 128:(jj + 1) * 128]
                P.add("pe", MM(ob, lh, rh, jj == jjs[0], jj == 4), reads=[lh, rh], writes=[ob])

        def TP(m):
            ab = m % 2
            qs = slice(m * 128, (m + 1) * 128)
            tb = ps[:, 2 + ab, :]
            for j in range(4):
                lh = ao[:, ab, j * 128:(j + 1) * 128]
                P.add("pe", MM(tb[:, j * 128:(j + 1) * 128], lh, idn_m, True, True),
                      reads=[lh, idn_m], writes=[tb[:, j * 128:(j + 1) * 128]])
            P.add("act", ACT(mix_v[:, 0:4, qs], tb.rearrange("p (j t) -> p j t", j=4), AF.Copy),
                  reads=[tb], writes=[mix_v[:, 0:4, qs]])

        seq = [(m, h) for m in range(8) for h in range(8)]
        S(*seq[0])
        S(*seq[1])
        for i, (m, h) in enumerate(seq):
            ab = m % 2
            qs = slice(m * 128, (m + 1) * 128)
            if i + 2 < len(seq):
                S(*seq[i + 2])
            PV(m, h)
            if h == 0 and m == 3:
                flush_deferred()
            if h % 4 == 3:
                g = h // 4
                ob3 = ps[:, g, 0:260].rearrange("p (h d) -> p h d", h=4)
                rcv = rc[:, ab, g * 4:(g + 1) * 4]
                P.add("dve", RCP(rcv, ob3[:, :, 64]), reads=[ob3[:, :, 64]], writes=[rcv])
                aov = ao[:, ab, g * 256:(g + 1) * 256]
                P.add("dve", TTO(aov.rearrange("p (h d) -> p h d", h=4), ob3[:, :, 0:64],
                                 rcv.unsqueeze(2).to_broadcast([128, 4, 64]), MUL),
                      reads=[ps[:, g, 0:260], rcv], writes=[aov])
            if h == 1 and m >= 1:
                TP(m - 1)
        TP(7)
        P.add("pool", CPY(KH[:, l], k_v[:, :, 512:1024]), reads=[k_v[:, :, 512:1024]], writes=[KH[:, l]])
        P.add("pool", CPY(VH[:, l, :], U[:, 8192 + 2080:8192 + 4160]),
              reads=[U[:, 8192 + 2080:8192 + 4160]], writes=[VH[:, l, :]])

    def phase_B():
        for dh in range(2):
            W = cur_w().rearrange("p (c e) -> p c e", c=8)
            for dcl in range(4):
                dc = dh * 4 + dcl
                for s in range(2):
                    sl = slice(s * 512, (s + 1) * 512)
                    Pb = bigP()
                    mm_group(Pb, W, dcl * 128, mix_v, sl)
                    xs = x_sb[:, dc, sl]
                    P.add("dve", TTO(xs, Pb, xs, ADD), reads=[Pb, xs], writes=[xs])
            release_w()

    def phase_M1():
        for fg in range(8):
            W = cur_w().rearrange("p (c e) -> p c e", c=8)
            for s in range(2):
                for j in range(4):
                    fc = fg * 4 + j
                    sl = slice(s * 512, (s + 1) * 512)
                    Pb = bigP()
                    mm_group(Pb, W, j * 128, h_sb, sl)
                    r = rot("rl", rl)
                    P.add("act", ACT(r, Pb, AF.Relu), reads=[Pb], writes=[r])
                    P.add("dve", TTO(f_v[:, s, fc, :], r, r, MUL), reads=[r], writes=[f_v[:, s, fc, :]])
            release_w()

    def phase_M2():
        for dc in range(8):
            W = cur_w().rearrange("p (f d) -> p f d", f=32)
            for s in range(2):
                sl = slice(s * 512, (s + 1) * 512)
                Pb = bigP()
                for fc in range(32):
                    P.add("pe", MM(Pb, W[:, fc, :], f_v[:, s, fc, :], fc == 0, fc == 31),
                          reads=[W[:, fc, :], f_v[:, s, fc, :]], writes=[Pb])
                xs = x_sb[:, dc, sl]
                P.add("dve", TTO(xs, Pb, xs, ADD), reads=[Pb, xs], writes=[xs])
            release_w()

    import os
    kstop = int(os.environ.get("KSTOP", "99"))
    for tile in range(NT):
        t0 = tile * TT
        src = xT_d[:, :, t0:t0 + TT].rearrange("c p t -> p c t")
        P.add("sp", DMA(x_sb[:, :, :], src), writes=[x_sb[:, :, :]], dma="xin")
        for l in range(NLAY):
            cb0 = l * CP_L
            P.tag = "N1"
            if kstop >= 1: phase_norm(cb0 + 0)
            P.tag = "Q"
            if kstop >= 2: phase_qk(q_v, col(cb0 + 16))
            P.tag = "K"
            if kstop >= 3: phase_qk(k_v, col(cb0 + 17))
            P.tag = "V"
            if kstop >= 4: phase_v()
            P.tag = "CP"
            if kstop >= 5: phase_convpool(l, tile)
            P.tag = "AT"
            if kstop >= 6: phase_attn(l, tile)
            flush_deferred()
            P.tag = "B"
            if kstop >= 7: phase_B()
            P.tag = "N2"
            if kstop >= 8: phase_norm(cb0 + 8)
            P.tag = "M1"
            if kstop >= 9: phase_M1()
            P.tag = "M2"
            if kstop >= 10: phase_M2()
        dst = y_d[:, :, t0:t0 + TT].rearrange("c p t -> p c t")
        P.add("sp", DMA(dst, x_sb[:, :, :]), reads=[x_sb[:, :, :]], dma="xout")
    P.add("sp", None, writes=[x_sb[:, :, :]])

    dkeys = P.finalize()
    sems = {}
    for e in ENGS:
        sems[("e", e)] = es.enter_context(nc.semaphore("s_" + e))
    for k in dkeys:
        sems[("d", k)] = es.enter_context(nc.semaphore("d_" + k))
    by_eng = {e: [] for e in ENGS}
    for op in P.ops:
        by_eng[op.eng].append(op)

    def emitter(en):
        def f(e):
            mysem = sems[("e", en)]
            for op in by_eng[en]:
                for key, val in op.waits:
                    e.wait_ge(sems[key], val)
                if op.fn is None:
                    continue
                ins = op.fn(e)
                if op.dma is not None:
                    ins.then_inc(sems[("d", op.dma)], 16)
                elif op.signal:
                    ins.then_inc(mysem, 1)
        return f

    with nc.Block() as block:
        block.tensor(emitter("pe"))
        block.scalar(emitter("act"))
        block.vector(emitter("dve"))
        block.gpsimd(emitter("pool"))
        block.sync(emitter("sp"))
    es.close()
    if P.tags is not None:
        nc._ktags = P.tags
    return nc


def prep_weights(norm1_g, w_in, q_norm_g, k_norm_g, rel_bias, conv_w, pool_w, pool_scale, w_out, norm2_g,
                 w_mlp1, w_mlp2):
    wst = np.empty((NL, NSLOT, 128, 4096), np.float32)
    for l in range(NL):
        wi = np.asarray(w_in[l], np.float32).reshape(8, 128, 5, 512)
        wst[l, 0:5] = wi.transpose(2, 1, 0, 3).reshape(5, 128, 4096)
        wo = np.asarray(w_out[l], np.float32).reshape(8, 128, 2, 512)
        wst[l, 5:7] = wo.transpose(2, 1, 0, 3).reshape(2, 128, 4096)
        w1 = np.asarray(w_mlp1[l], np.float32).reshape(8, 128, 8, 512)
        wst[l, 7:15] = w1.transpose(2, 1, 0, 3).reshape(8, 128, 4096)
        w2 = np.asarray(w_mlp2[l], np.float32).reshape(32, 128, 8, 128)
        wst[l, 15:23] = w2.transpose(2, 1, 0, 3).reshape(8, 128, 4096)
    cpack = np.zeros((128, NCOL), np.float32)
    p = np.arange(128)
    for l in range(NL):
        b = l * CP_L
        cpack[:, b:b + 8] = np.asarray(norm1_g[l], np.float32).reshape(8, 128).T
        cpack[:, b + 8:b + 16] = np.asarray(norm2_g[l], np.float32).reshape(8, 128).T
        cpack[:, b + 16] = np.asarray(q_norm_g[l], np.float32)[p % 64]
        cpack[:, b + 17] = np.asarray(k_norm_g[l], np.float32)[p % 64]
        cw = np.asarray(conv_w[l], np.float32)
        for cc in range(2):
            for i in range(3):
                cpack[:, b + 18 + cc * 3 + i] = cw[i, cc * 128 + p]
            cpack[:, b + 24 + cc] = np.asarray(pool_scale[l], np.float32)[cc * 128 + p]
        cpack[:, CP_CB + l * 8:CP_CB + l * 8 + 8] = np.asarray(rel_bias[l], np.float32)[:, 256][None, :]
    wins = np.array([[2, 4], [8, 16]])
    for cc in range(2):
        wv = wins[cc][p // 64].astype(np.float32)
        cpack[:, CP_INVW + cc] = np.float32(1.0) / wv
        for t in range(16):
            cpack[:, CP_INVC + cc * 16 + t] = np.float32(1.0) / np.minimum(np.float32(t + 1), wv)
    cmats = np.zeros((128, 3, 128), np.float32)
    cmats[:, 0, :] = 1.0 / 1024.0
    cmats[0:64, 1, 0:64] = 1.0 / 64.0
    cmats[64:128, 1, 64:128] = 1.0 / 64.0
    cmats[:, 2, :] = np.eye(128, dtype=np.float32)
    pw = np.zeros((128, NL * 2, 128), np.float32)
    for l in range(NL):
        for cc in range(2):
            for half in range(2):
                g = 2 * cc + half
                pw[half * 64:(half + 1) * 64, l * 2 + cc, half * 64:(half + 1) * 64] = np.asarray(pool_w[l][g], np.float32)
    kk = np.arange(128)[:, None]
    r = np.arange(128)[None, :]
    idx3 = np.minimum(r - kk + 256, 256)
    idx4 = r - kk + 128
    rb = np.asarray(rel_bias, np.float32)
    rbt = np.empty((NL, 128, 8, 2, 128), np.float32)
    rbt[:, :, :, 0, :] = rb[:, :, idx3].transpose(0, 2, 1, 3)
    rbt[:, :, :, 1, :] = rb[:, :, idx4].transpose(0, 2, 1, 3)
    return wst, cpack, cmats, pw, np.ascontiguousarray(rbt.reshape(NL, 128, 2048))


_NC_CACHE = {}


def kernel(x, norm1_g, w_in, q_norm_g, k_norm_g, rel_bias, conv_w, pool_w, pool_scale, w_out, norm2_g,
           w_mlp1, w_mlp2):
    x = np.asarray(x, np.float32)
    B, S, _ = x.shape
    NT = S // TT
    wst, cpack, cmats, pw, rbt = prep_weights(norm1_g, w_in, q_norm_g, k_norm_g, rel_bias, conv_w, pool_w,
                                              pool_scale, w_out, norm2_g, w_mlp1, w_mlp2)
    key = (NT, NL)
    if key not in _NC_CACHE:
        _NC_CACHE[key] = build_nc(NT, NL)
    nc = _NC_CACHE[key]
    in_maps = []
    for b in range(B):
        xT = np.ascontiguousarray(x[b].T).reshape(8, 128, S)
        in_maps.append({"xT": xT, "wst": wst, "cpack": cpack, "cmats": cmats, "pw": pw, "rbt": rbt})
    res = run_bass_kernel_spmd(nc, in_maps, core_ids=list(range(B)))
    out = np.empty((B, S, D), np.float32)
    for b in range(B):
        out[b] = np.asarray(res.results[b]["yT"], np.float32).reshape(D, S).T
    return out
```

```python
import os
import numpy as np
from contextlib import ExitStack
import concourse.bass as bass
import concourse.mybir as mybir
from concourse.bass_utils import run_bass_kernel_spmd

F32 = mybir.dt.float32
BF16 = mybir.dt.bfloat16
AF = mybir.ActivationFunctionType
ALU = mybir.AluOpType

D = 1024
SEQ = 4096
NL = 4
TT = 1024
NSLOT = 23
RING = 3
ENGS = ["pe", "act", "dve", "pool", "sp"]
CP_L = 26
CP_INVW = NL * CP_L
CP_INVC = CP_INVW + 2
CP_CB = CP_INVC + 32
NCOL = CP_CB + NL * 8


class Op:
    __slots__ = ("eng", "fn", "deps", "dma", "signal", "seq", "waits")

    def __init__(self, eng, fn, deps, dma):
        self.eng = eng; self.fn = fn; self.deps = deps; self.dma = dma
        self.signal = False; self.seq = 0; self.waits = ()


class Prog:
    def __init__(self):
        self.ops = []
        self.recs = {}
        self.tags = [] if os.environ.get("KTAGS") else None
        self.tag = "pro"

    @staticmethod
    def _box(ap):
        t = ap.tensor
        row = 1
        for d in list(t.shape)[1:]:
            row *= int(d)
        e = 4 if t.dtype == F32 else 2
        off = int(ap.offset)
        p0 = off // row
        f0 = off % row
        dims = ap.ap
        pc = int(dims[0][1])
        ext = 0
        for st, cn in dims[1:]:
            ext += (int(cn) - 1) * int(st)
        return (t.name, p0, p0 + pc, f0 * e, (f0 + ext + 1) * e)

    def add(self, eng, fn, reads=(), writes=(), dma=None):
        oid = len(self.ops)
        ops = self.ops
        rb = [self._box(a) for a in reads]
        wb = [self._box(a) for a in writes]
        need = set()
        for (n, p0, p1, b0, b1) in rb:
            for r in self.recs.get(n, ()):
                if r[5] and r[0] < p1 and p0 < r[1] and r[2] < b1 and b0 < r[3]:
                    need.add((r[4], 0))
        for (n, p0, p1, b0, b1) in wb:
            for r in self.recs.get(n, ()):
                if r[0] < p1 and p0 < r[1] and r[2] < b1 and b0 < r[3]:
                    need.add((r[4], 1))
        deps = set()
        for pid, kind in need:
            Pp = ops[pid]
            if Pp.dma is None and Pp.eng == eng:
                if eng == "pe":
                    continue
                if dma is None and kind == 1:
                    continue
            deps.add(pid)
        ops.append(Op(eng, fn, deps, dma))
        if self.tags is not None:
            self.tags.append((eng, self.tag))
        for (n, p0, p1, b0, b1) in wb:
            lst = self.recs.setdefault(n, [])
            lst[:] = [r for r in lst if not (p0 <= r[0] and r[1] <= p1 and b0 <= r[2] and r[3] <= b1)]
            lst.append([p0, p1, b0, b1, oid, True])
        for (n, p0, p1, b0, b1) in rb:
            lst = self.recs.setdefault(n, [])
            for r in lst:
                if (not r[5]) and r[0] == p0 and r[1] == p1 and r[2] == b0 and r[3] == b1:
                    Pp = ops[r[4]]
                    if Pp.eng == eng and Pp.dma is None and dma is None:
                        r[4] = oid
                        break
            else:
                lst.append([p0, p1, b0, b1, oid, False])
        return oid

    def finalize(self):
        ops = self.ops
        for op in ops:
            for pid in op.deps:
                ops[pid].signal = True
        cnt = {e: 0 for e in ENGS}
        dcnt = {}
        for op in ops:
            if op.dma is not None:
                dcnt[op.dma] = dcnt.get(op.dma, 0) + 16
                op.seq = dcnt[op.dma]
            elif op.signal:
                cnt[op.eng] += 1
                op.seq = cnt[op.eng]
        waited = {e: {} for e in ENGS}
        for op in ops:
            w = {}
            for pid in op.deps:
                Pp = ops[pid]
                key = ("d", Pp.dma) if Pp.dma is not None else ("e", Pp.eng)
                if w.get(key, 0) < Pp.seq:
                    w[key] = Pp.seq
            wl = []
            we = waited[op.eng]
            for key, val in w.items():
                if we.get(key, 0) < val:
                    we[key] = val
                    wl.append((key, val))
            op.waits = wl
        return sorted(dcnt.keys())


def MM(out, lhsT, rhs, st, sp):
    return lambda e: e.matmul(out, lhsT=lhsT, rhs=rhs, start=st, stop=sp)


def ACT(out, in_, func, **kw):
    return lambda e: e.activation(out=out, in_=in_, func=func, **kw)


def TTO(out, in0, in1, op):
    return lambda e: e.tensor_tensor(out=out, in0=in0, in1=in1, op=op)


def TS(out, in0, s1, op0):
    return lambda e: e.tensor_scalar(out=out, in0=in0, scalar1=s1, scalar2=None, op0=op0)


def STT(out, in0, scalar, in1, op0, op1):
    return lambda e: e.scalar_tensor_tensor(out=out, in0=in0, scalar=scalar, in1=in1, op0=op0, op1=op1)


def CPY(out, in_):
    return lambda e: e.tensor_copy(out=out, in_=in_)


def MSET(ap, v):
    return lambda e: e.memset(ap, v)


def RCP(out, in_):
    return lambda e: e.reciprocal(out=out, in_=in_)


def DMA(out, in_):
    return lambda e: e.dma_start(out=out, in_=in_)


def build_nc(NT=4, NLAY=NL):
    nc = bass.Bass("TRN2", target_bir_lowering=False)
    S_ = NT * TT
    xT_d = nc.dram_tensor("xT", [8, 128, S_], F32, kind="ExternalInput").ap()
    wst_d = nc.dram_tensor("wst", [NL, NSLOT, 128, 4096], F32, kind="ExternalInput").ap()
    cpack_d = nc.dram_tensor("cpack", [128, NCOL], F32, kind="ExternalInput").ap()
    cmats_d = nc.dram_tensor("cmats", [128, 3, 128], F32, kind="ExternalInput").ap()
    pw_d = nc.dram_tensor("pw", [128, NL * 2, 128], F32, kind="ExternalInput").ap()
    rbt_d = nc.dram_tensor("rbt", [NL, 128, 2048], F32, kind="ExternalInput").ap()
    y_d = nc.dram_tensor("yT", [8, 128, S_], F32, kind="ExternalOutput").ap()

    P = Prog()
    es = ExitStack()

    def sb(name, shape, dt):
        return es.enter_context(nc.sbuf_tensor(name, shape, dt))

    x_sb = sb("x_sb", [128, 8, TT], F32)
    h_sb = sb("h_sb", [128, 8, TT], BF16)
    U = sb("U", [128, 32768], BF16)
    KH = sb("KH", [128, NL, 4, 512], BF16)
    VH = sb("VH", [128, NL, 2080], BF16)
    WR = sb("WR", [128, RING, 4096], BF16)
    EB = sb("EB", [128, NL, 8, 256], BF16)
    rstd = sb("rstd", [128, 2, 512], F32)
    sqb = sb("sqb", [128, 2, 512], BF16)
    rl = sb("rl", [128, 2, 512], F32)
    PT = sb("PT", [128, 3, 640], BF16)
    ao = sb("ao", [128, 2, 512], BF16)
    rc = sb("rc", [128, 2, 8], F32)
    cpk = sb("cpk", [128, NCOL], F32)
    cm = sb("cm", [128, 3, 128], BF16)
    PW = sb("PW", [128, NL * 2, 128], BF16)
    zh = sb("zh", [128, NL, 2, 2], F32)
    uh = sb("uh", [128, NL, 2, 16], F32)
    epsc = sb("epsc", [128, 1], F32)
    ps = es.enter_context(nc.psum_tensor("ps", [128, 8, 512], F32))

    q_v = U[:, 0:4096].rearrange("p (j t) -> p j t", j=4)
    k_v = U[:, 4096:8192].rearrange("p (j t) -> p j t", j=4)
    v_flat = U[:, 8192:8192 + 4160]
    v_v = v_flat.rearrange("p (b h d) -> p b h d", b=8, h=8)
    mix_v = U[:, 12352:12352 + 8192].rearrange("p (j t) -> p j t", j=8)
    xsq = U[:, 20544:20544 + 4096].rearrange("p (c t) -> p c t", c=8)
    gcz = U[:, 20544:20544 + 2056].bitcast(F32).rearrange("p (c t) -> p c t", c=2)
    gb_v = U[:, 22600:22600 + 2048].bitcast(F32).rearrange("p (c t) -> p c t", c=2)
    acc1 = U[:, 24648:24648 + 1024].bitcast(F32)
    mbf2 = U[:, 25672:25672 + 1024].rearrange("p (c t) -> p c t", c=2)
    ubuf = U[:, 26696:26696 + 2112].bitcast(F32).rearrange("p (c t) -> p c t", c=2)
    sA = U[:, 28808:28808 + 1056].bitcast(F32)
    sB = U[:, 29864:29864 + 1056].bitcast(F32)
    mbf_a = U[:, 30920:30920 + 1024].rearrange("p (c t) -> p c t", c=2)
    f_v = U[:, :].rearrange("p (s f t) -> p s f t", s=2, f=32)

    ones_m = cm[:, 0, :]
    blk_m = cm[:, 1, :]
    idn_m = cm[:, 2, :]
    eps_ap = epsc[:, 0:1]

    def col(i):
        return cpk[:, i:i + 1]

    MUL = ALU.mult
    ADD = ALU.add
    SUB = ALU.subtract

    st = {"big": 0, "aux": 0, "pool": 0, "rstd": 0, "sq": 0, "rl": 0, "use": 0, "issued": 0}
    n_uses = NT * NLAY * NSLOT

    def bigP():
        b = st["big"] % 4
        st["big"] += 1
        return ps[:, b, :]

    def auxP():
        b = 4 + st["aux"] % 2
        st["aux"] += 1
        return ps[:, b, :]

    def poolP():
        b = 6 + st["pool"] % 2
        st["pool"] += 1
        return ps[:, b, :]

    def rot(name, tensor):
        i = st[name] % 2
        st[name] += 1
        return tensor[:, i, :]

    def issue_w():
        u = st["issued"]
        if u >= n_uses:
            return
        st["issued"] += 1
        si = u % NSLOT
        l = (u // NSLOT) % NLAY
        slot = u % RING
        P.add("pool", DMA(WR[:, slot, :], wst_d[l, si]), writes=[WR[:, slot, :]], dma="w%d" % slot)

    def cur_w(k=0):
        return WR[:, (st["use"] + k) % RING, :]

    def release_w(n=1):
        for _ in range(n):
            st["use"] += 1
            issue_w()

    P.add("sp", DMA(cpk[:, :], cpack_d), writes=[cpk[:, :]], dma="c0")
    P.add("pool", DMA(cm[:, :, :], cmats_d), writes=[cm[:, :, :]], dma="c1")
    P.add("pool", DMA(PW[:, :, :], pw_d), writes=[PW[:, :, :]], dma="c2")
    for _ in range(RING):
        issue_w()
    P.add("dve", MSET(epsc[:, :], 1e-6), writes=[epsc[:, :]])
    P.add("dve", MSET(zh[:, :, :, :], 0.0), writes=[zh[:, :, :, :]])
    P.add("dve", MSET(uh[:, :, :, :], 0.0), writes=[uh[:, :, :, :]])
    for l in range(NLAY):
        stg = U[:, l * 4096:(l + 1) * 4096].bitcast(F32)
        stg3 = stg.rearrange("p (h e) -> p h e", h=8)
        P.add("sp", DMA(stg, rbt_d[l]), writes=[stg], dma="r%d" % l)
        cb = cpk[:, CP_CB + l * 8:CP_CB + l * 8 + 8]
        P.add("dve", TTO(stg3, stg3, cb.unsqueeze(2).to_broadcast([128, 8, 256]), SUB),
              reads=[stg, cb], writes=[stg])
        ebl = EB[:, l].rearrange("p h e -> p (h e)")
        P.add("act", ACT(ebl, stg, AF.Exp), reads=[stg], writes=[ebl])
        ebm = EB[64:128, l].rearrange("p h (j r) -> p h j r", j=2)[:, :, 1, 0:64]
        P.add("dve", MSET(ebm, 0.0), writes=[ebm])

    deferred = []

    def flush_deferred():
        while deferred:
            deferred.pop(0)()

    def phase_norm(gbase):
        for s in range(2):
            sl = slice(s * 512, (s + 1) * 512)
            for c in range(8):
                xs = x_sb[:, c, sl]
                if True:
                    P.add("act", ACT(xsq[:, c, :], xs, AF.Square), reads=[xs], writes=[xsq[:, c, :]])
                else:
                    P.add("dve", TTO(xsq[:, c, :], xs, xs, MUL), reads=[xs], writes=[xsq[:, c, :]])
            a = auxP()
            for c in range(8):
                P.add("pe", MM(a, ones_m, xsq[:, c, :], c == 0, c == 7), reads=[ones_m, xsq[:, c, :]], writes=[a])
            r = rot("rstd", rstd)
            P.add("act", ACT(r, a, AF.Ln, bias=eps_ap, scale=1.0), reads=[a, eps_ap], writes=[r])
            P.add("act", ACT(r, r, AF.Exp, scale=-0.5), reads=[r], writes=[r])
            for c in range(8):
                g = col(gbase + c)
                P.add("dve", STT(h_sb[:, c, sl], x_sb[:, c, sl], g, r, MUL, MUL),
                      reads=[x_sb[:, c, sl], g, r], writes=[h_sb[:, c, sl]])

    def mm_group(Pb, W, c0, rhs_t, sl):
        for c in range(8):
            P.add("pe", MM(Pb, W[:, c, c0:c0 + 128], rhs_t[:, c, sl], c == 0, c == 7),
                  reads=[W[:, c, c0:c0 + 128], rhs_t[:, c, sl]], writes=[Pb])

    def phase_qk(dst, gcol):
        W = cur_w().rearrange("p (c e) -> p c e", c=8)
        pend = []

        def post(j, s, Pb):
            sl = slice(s * 512, (s + 1) * 512)
            sq = rot("sq", sqb)
            P.add("act", ACT(sq, Pb, AF.Square), reads=[Pb], writes=[sq])
            a = auxP()
            P.add("pe", MM(a, blk_m, sq, True, True), reads=[blk_m, sq], writes=[a])
            r = rot("rstd", rstd)
            P.add("act", ACT(r, a, AF.Ln, bias=eps_ap, scale=1.0), reads=[a, eps_ap], writes=[r])
            P.add("act", ACT(r, r, AF.Exp, scale=-0.5), reads=[r], writes=[r])
            P.add("dve", STT(dst[:, j, sl], Pb, gcol, r, MUL, MUL), reads=[Pb, gcol, r], writes=[dst[:, j, sl]])

        for s in range(2):
            for j in range(4):
                Pb = bigP()
                mm_group(Pb, W, j * 128, h_sb, slice(s * 512, (s + 1) * 512))
                if pend:
                    post(*pend.pop())
                pend.append((j, s, Pb))
        post(*pend.pop())
        release_w()

    def phase_v():
        W = cur_w().rearrange("p (c e) -> p c e", c=8)
        onesc = v_v[:, :, :, 64:65]
        P.add("dve", MSET(onesc, 1.0), writes=[onesc])
        for b in range(8):
            Pb = bigP()
            for c in range(8):
                lh = h_sb[:, c, b * 128:(b + 1) * 128]
                P.add("pe", MM(Pb, lh, W[:, c, :], c == 0, c == 7), reads=[lh, W[:, c, :]], writes=[Pb])
            dst = v_v[:, b, :, 0:64]
            src = Pb.rearrange("p (h d) -> p h d", h=8)
            if b % 2 == 0:
                P.add("dve", CPY(dst, src), reads=[Pb], writes=[dst])
            else:
                P.add("act", ACT(dst, src, AF.Copy), reads=[Pb], writes=[dst])
        release_w()

    def phase_convpool(l, tile):
        W3 = cur_w(0).rearrange("p (c e) -> p c e", c=8)
        W4 = cur_w(1).rearrange("p (c e) -> p c e", c=8)
        cb0 = l * CP_L

        def cw(cc, i):
            return col(cb0 + 18 + cc * 3 + i)

        for s in range(2):
            sl = slice(s * 512, (s + 1) * 512)
            mbf = mbf_a if s == 0 else mbf2
            P.add("act", ACT(gcz[:, :, 0:2], zh[:, l], AF.Copy), reads=[zh[:, l]], writes=[gcz[:, :, 0:2]])
            P.add("act", ACT(ubuf[:, :, 0:16], uh[:, l], AF.Copy), reads=[uh[:, l]], writes=[ubuf[:, :, 0:16]])
            for cc in range(2):
                Pb = bigP()
                mm_group(Pb, W3, cc * 128, h_sb, sl)
                P.add("act", ACT(gb_v[:, cc, :], Pb, AF.Copy), reads=[Pb], writes=[gb_v[:, cc, :]])
            for cc in range(2):
                Pb = bigP()
                mm_group(Pb, W3, 256 + cc * 128, h_sb, sl)
                P.add("act", ACT(gcz[:, cc, 2:514], Pb, AF.Copy), reads=[Pb], writes=[gcz[:, cc, 2:514]])
            for cc in range(2):
                Pb = bigP()
                mm_group(Pb, W4, cc * 128, h_sb, sl)
                z = gcz[:, cc, 2:514]
                P.add("dve", TTO(z, Pb, z, MUL), reads=[Pb, z], writes=[z])
            for cc in range(2):
                Pb = bigP()
                mm_group(Pb, W4, 256 + cc * 128, h_sb, sl)
                P.add("act", ACT(ubuf[:, cc, 16:528], Pb, AF.Copy), reads=[Pb], writes=[ubuf[:, cc, 16:528]])
            P.add("act", ACT(zh[:, l], gcz[:, :, 512:514], AF.Copy), reads=[gcz[:, :, 512:514]], writes=[zh[:, l]])
            P.add("act", ACT(uh[:, l], ubuf[:, :, 512:528], AF.Copy), reads=[ubuf[:, :, 512:528]], writes=[uh[:, l]])
            for cc in range(2):
                P.add("act", ACT(acc1, gcz[:, cc, 2:514], AF.Copy, scale=cw(cc, 2)),
                      reads=[gcz[:, cc, 2:514], cw(cc, 2)], writes=[acc1])
                P.add("dve", STT(acc1, gcz[:, cc, 1:513], cw(cc, 1), acc1, MUL, ADD),
                      reads=[gcz[:, cc, 1:513], cw(cc, 1), acc1], writes=[acc1])
                P.add("dve", STT(acc1, gcz[:, cc, 0:512], cw(cc, 0), acc1, MUL, ADD),
                      reads=[gcz[:, cc, 0:512], cw(cc, 0), acc1], writes=[acc1])
                P.add("dve", TTO(mix_v[:, 4 + cc, sl], acc1, gb_v[:, cc, :], MUL),
                      reads=[acc1, gb_v[:, cc, :]], writes=[mix_v[:, 4 + cc, sl]])
            for cc in range(2):
                Uc = ubuf[:, cc, :]
                P.add("dve", TTO(sA[:, 1:528], Uc[:, 1:528], Uc[:, 0:527], ADD),
                      reads=[Uc], writes=[sA[:, 1:528]])
                P.add("dve", TTO(sB[:, 3:528], sA[:, 3:528], sA[:, 1:526], ADD),
                      reads=[sA[:, 1:528]], writes=[sB[:, 3:528]])
                if cc == 1:
                    P.add("dve", TTO(sA[:, 7:528], sB[:, 7:528], sB[:, 3:524], ADD),
                          reads=[sB[:, 3:528]], writes=[sA[:, 7:528]])
                    P.add("dve", TTO(sB[:, 15:528], sA[:, 15:528], sA[:, 7:520], ADD),
                          reads=[sA[:, 7:528]], writes=[sB[:, 15:528]])
                for (pr, src) in ((slice(0, 64), sA), (slice(64, 128), sB)):
                    iw = cpk[pr, CP_INVW + cc:CP_INVW + cc + 1]
                    P.add("dve", STT(mbf[pr, cc, :], src[pr, 16:528], iw, Uc[pr, 16:528], MUL, SUB),
                          reads=[src[pr, 16:528], iw, Uc[pr, 16:528]], writes=[mbf[pr, cc, :]])
                    if tile == 0 and s == 0:
                        ic = cpk[pr, CP_INVC + cc * 16:CP_INVC + cc * 16 + 16]
                        P.add("dve", TTO(src[pr, 16:32], src[pr, 16:32], ic, MUL),
                              reads=[src[pr, 16:32], ic], writes=[src[pr, 16:32]])
                        P.add("dve", TTO(mbf[pr, cc, 0:16], src[pr, 16:32], Uc[pr, 16:32], SUB),
                              reads=[src[pr, 16:32], Uc[pr, 16:32]], writes=[mbf[pr, cc, 0:16]])

            def pool_mm(l=l, sl=sl, s=s, mbf=mbf):
                for cc in range(2):
                    pb = poolP()
                    P.add("pe", MM(pb, PW[:, l * 2 + cc, :], mbf[:, cc, :], True, True),
                          reads=[PW[:, l * 2 + cc, :], mbf[:, cc, :]], writes=[pb])
                    sc = col(cb0 + 24 + cc)
                    P.add("act", ACT(mix_v[:, 6 + cc, sl], pb, AF.Copy, scale=sc),
                          reads=[pb, sc], writes=[mix_v[:, 6 + cc, sl]])
            deferred.append(pool_mm)
        release_w(2)

    def phase_attn(l, tile):
        def S(m, h):
            jjs = [jj for jj in range(5) if (m - 4 + jj >= 0 or tile > 0)]
            lo = jjs[0] * 128
            j, hh = divmod(h, 2)
            pr = slice(hh * 64, hh * 64 + 64)
            scb = ps[:, 4 + 2 * hh:6 + 2 * hh, :].rearrange("p a b -> p (a b)")
            qs = slice(m * 128, (m + 1) * 128)
            for jj in jjs:
                kt = m - 4 + jj
                if kt >= 0:
                    lh = k_v[pr, j, kt * 128:(kt + 1) * 128]
                else:
                    lh = KH[pr, l, j, (kt + 4) * 128:(kt + 5) * 128]
                o = scb[:, jj * 128:(jj + 1) * 128]
                P.add("pe", MM(o, lh, q_v[pr, j, qs], True, True), reads=[lh, q_v[pr, j, qs]], writes=[o])
            pti = (m * 8 + h) % 3
            ptb = PT[:, pti, :]
            P.add("act", ACT(ptb[:, lo:640], scb[:, lo:640], AF.Exp, scale=0.125),
                  reads=[scb[:, lo:640]], writes=[ptb[:, lo:640]])
            P.add("dve", TTO(ptb[:, 384:640], ptb[:, 384:640], EB[:, l, h, :], MUL),
                  reads=[ptb[:, 384:640], EB[:, l, h, :]], writes=[ptb[:, 384:640]])
            if 0 in jjs:
                P.add("dve", MSET(PT[0:64, pti, 64:128], 0.0), writes=[PT[0:64, pti, 64:128]])

        def PV(m, h):
            jjs = [jj for jj in range(5) if (m - 4 + jj >= 0 or tile > 0)]
            hh = h % 2
            ob = ps[:, h // 4, (h % 4) * 65:(h % 4) * 65 + 65]
            for jj in jjs:
                kt = m - 4 + jj
                if kt >= 0:
                    rh = v_v[:, kt, h, :]
                else:
                    rh = VH[:, l, (kt + 4) * 520 + h * 65:(kt + 4) * 520 + h * 65 + 65]
                lh = PT[:, (m * 8 + h) % 3, jj * 128:(jj + 1) * 128]
                P.add("pe", MM(ob, lh, rh, jj == jjs[0], jj == 4), reads=[lh, rh], writes=[ob])

        def TP(m):
            ab = m % 2
            qs = slice(m * 128, (m + 1) * 128)
            tb = ps[:, 2 + ab, :]
            for j in range(4):
                lh = ao[:, ab, j * 128:(j + 1) * 128]
                P.add("pe", MM(tb[:, j * 128:(j + 1) * 128], lh, idn_m, True, True),
                      reads=[lh, idn_m], writes=[tb[:, j * 128:(j + 1) * 128]])
            P.add("act", ACT(mix_v[:, 0:4, qs], tb.rearrange("p (j t) -> p j t", j=4), AF.Copy),
                  reads=[tb], writes=[mix_v[:, 0:4, qs]])

        seq = [(m, h) for m in range(8) for h in range(8)]
        S(*seq[0])
        S(*seq[1])
        for i, (m, h) in enumerate(seq):
            ab = m % 2
            qs = slice(m * 128, (m + 1) * 128)
            if i + 2 < len(seq):
                S(*seq[i + 2])
            PV(m, h)
            if h == 0 and m == 3:
                flush_deferred()
            if h % 4 == 3:
                g = h // 4
                ob3 = ps[:, g, 0:260].rearrange("p (h d) -> p h d", h=4)
                rcv = rc[:, ab, g * 4:(g + 1) * 4]
                P.add("dve", RCP(rcv, ob3[:, :, 64]), reads=[ob3[:, :, 64]], writes=[rcv])
                aov = ao[:, ab, g * 256:(g + 1) * 256]
                P.add("dve", TTO(aov.rearrange("p (h d) -> p h d", h=4), ob3[:, :, 0:64],
                                 rcv.unsqueeze(2).to_broadcast([128, 4, 64]), MUL),
                      reads=[ps[:, g, 0:260], rcv], writes=[aov])
            if h == 1 and m >= 1:
                TP(m - 1)
        TP(7)
        P.add("act", ACT(KH[:, l], k_v[:, :, 512:1024], AF.Copy), reads=[k_v[:, :, 512:1024]], writes=[KH[:, l]])
        P.add("act", ACT(VH[:, l, :], U[:, 8192 + 2080:8192 + 4160], AF.Copy),
              reads=[U[:, 8192 + 2080:8192 + 4160]], writes=[VH[:, l, :]])

    def phase_B():
        for dh in range(2):
            W = cur_w().rearrange("p (c e) -> p c e", c=8)
            for dcl in range(4):
                dc = dh * 4 + dcl
                for s in range(2):
                    sl = slice(s * 512, (s + 1) * 512)
                    Pb = bigP()
                    mm_group(Pb, W, dcl * 128, mix_v, sl)
                    xs = x_sb[:, dc, sl]
                    P.add("dve", TTO(xs, Pb, xs, ADD), reads=[Pb, xs], writes=[xs])
            release_w()

    def phase_M1():
        for fg in range(8):
            W = cur_w().rearrange("p (c e) -> p c e", c=8)
            for s in range(2):
                for j in range(4):
                    fc = fg * 4 + j
                    sl = slice(s * 512, (s + 1) * 512)
                    Pb = bigP()
                    mm_group(Pb, W, j * 128, h_sb, sl)
                    r = rot("rl", rl)
                    P.add("act", ACT(r, Pb, AF.Relu), reads=[Pb], writes=[r])
                    P.add("dve", TTO(f_v[:, s, fc, :], r, r, MUL), reads=[r], writes=[f_v[:, s, fc, :]])
            release_w()

    def phase_M2():
        for dc in range(8):
            W = cur_w().rearrange("p (f d) -> p f d", f=32)
            for s in range(2):
                sl = slice(s * 512, (s + 1) * 512)
                Pb = bigP()
                for fc in range(32):
                    P.add("pe", MM(Pb, W[:, fc, :], f_v[:, s, fc, :], fc == 0, fc == 31),
                          reads=[W[:, fc, :], f_v[:, s, fc, :]], writes=[Pb])
                xs = x_sb[:, dc, sl]
                P.add("dve", TTO(xs, Pb, xs, ADD), reads=[Pb, xs], writes=[xs])
            release_w()

    import os
    kstop = int(os.environ.get("KSTOP", "99"))
    for tile in range(NT):
        t0 = tile * TT
        src = xT_d[:, :, t0:t0 + TT].rearrange("c p t -> p c t")
        P.add("sp", DMA(x_sb[:, :, :], src), writes=[x_sb[:, :, :]], dma="xin")
        for l in range(NLAY):
            cb0 = l * CP_L
            P.tag = "N1"
            if kstop >= 1: phase_norm(cb0 + 0)
            P.tag = "Q"
            if kstop >= 2: phase_qk(q_v, col(cb0 + 16))
            P.tag = "K"
            if kstop >= 3: phase_qk(k_v, col(cb0 + 17))
            P.tag = "V"
            if kstop >= 4: phase_v()
            P.tag = "CP"
            if kstop >= 5: phase_convpool(l, tile)
            P.tag = "AT"
            if kstop >= 6: phase_attn(l, tile)
            flush_deferred()
            P.tag = "B"
            if kstop >= 7: phase_B()
            P.tag = "N2"
            if kstop >= 8: phase_norm(cb0 + 8)
            P.tag = "M1"
            if kstop >= 9: phase_M1()
            P.tag = "M2"
            if kstop >= 10: phase_M2()
        dst = y_d[:, :, t0:t0 + TT].rearrange("c p t -> p c t")
        P.add("sp", DMA(dst, x_sb[:, :, :]), reads=[x_sb[:, :, :]], dma="xout")
    P.add("sp", None, writes=[x_sb[:, :, :]])

    dkeys = P.finalize()
    sems = {}
    for e in ENGS:
        sems[("e", e)] = es.enter_context(nc.semaphore("s_" + e))
    for k in dkeys:
        sems[("d", k)] = es.enter_context(nc.semaphore("d_" + k))
    by_eng = {e: [] for e in ENGS}
    for op in P.ops:
        by_eng[op.eng].append(op)

    def emitter(en):
        def f(e):
            mysem = sems[("e", en)]
            for op in by_eng[en]:
                for key, val in op.waits:
                    e.wait_ge(sems[key], val)
                if op.fn is None:
                    continue
                ins = op.fn(e)
                if op.dma is not None:
                    ins.then_inc(sems[("d", op.dma)], 16)
                elif op.signal:
                    ins.then_inc(mysem, 1)
        return f

    with nc.Block() as block:
        block.tensor(emitter("pe"))
        block.scalar(emitter("act"))
        block.vector(emitter("dve"))
        block.gpsimd(emitter("pool"))
        block.sync(emitter("sp"))
    es.close()
    if P.tags is not None:
        nc._ktags = P.tags
    return nc


def prep_weights(norm1_g, w_in, q_norm_g, k_norm_g, rel_bias, conv_w, pool_w, pool_scale, w_out, norm2_g,
                 w_mlp1, w_mlp2):
    wst = np.empty((NL, NSLOT, 128, 4096), np.float32)
    for l in range(NL):
        wi = np.asarray(w_in[l], np.float32).reshape(8, 128, 5, 512)
        wst[l, 0:5] = wi.transpose(2, 1, 0, 3).reshape(5, 128, 4096)
        wo = np.asarray(w_out[l], np.float32).reshape(8, 128, 2, 512)
        wst[l, 5:7] = wo.transpose(2, 1, 0, 3).reshape(2, 128, 4096)
        w1 = np.asarray(w_mlp1[l], np.float32).reshape(8, 128, 8, 512)
        wst[l, 7:15] = w1.transpose(2, 1, 0, 3).reshape(8, 128, 4096)
        w2 = np.asarray(w_mlp2[l], np.float32).reshape(32, 128, 8, 128)
        wst[l, 15:23] = w2.transpose(2, 1, 0, 3).reshape(8, 128, 4096)
    cpack = np.zeros((128, NCOL), np.float32)
    p = np.arange(128)
    for l in range(NL):
        b = l * CP_L
        cpack[:, b:b + 8] = np.asarray(norm1_g[l], np.float32).reshape(8, 128).T
        cpack[:, b + 8:b + 16] = np.asarray(norm2_g[l], np.float32).reshape(8, 128).T
        cpack[:, b + 16] = np.asarray(q_norm_g[l], np.float32)[p % 64]
        cpack[:, b + 17] = np.asarray(k_norm_g[l], np.float32)[p % 64]
        cw = np.asarray(conv_w[l], np.float32)
        for cc in range(2):
            for i in range(3):
                cpack[:, b + 18 + cc * 3 + i] = cw[i, cc * 128 + p]
            cpack[:, b + 24 + cc] = np.asarray(pool_scale[l], np.float32)[cc * 128 + p]
        cpack[:, CP_CB + l * 8:CP_CB + l * 8 + 8] = np.asarray(rel_bias[l], np.float32)[:, 256][None, :]
    wins = np.array([[2, 4], [8, 16]])
    for cc in range(2):
        wv = wins[cc][p // 64].astype(np.float32)
        cpack[:, CP_INVW + cc] = np.float32(1.0) / wv
        for t in range(16):
            cpack[:, CP_INVC + cc * 16 + t] = np.float32(1.0) / np.minimum(np.float32(t + 1), wv)
    cmats = np.zeros((128, 3, 128), np.float32)
    cmats[:, 0, :] = 1.0 / 1024.0
    cmats[0:64, 1, 0:64] = 1.0 / 64.0
    cmats[64:128, 1, 64:128] = 1.0 / 64.0
    cmats[:, 2, :] = np.eye(128, dtype=np.float32)
    pw = np.zeros((128, NL * 2, 128), np.float32)
    for l in range(NL):
        for cc in range(2):
            for half in range(2):
                g = 2 * cc + half
                pw[half * 64:(half + 1) * 64, l * 2 + cc, half * 64:(half + 1) * 64] = np.asarray(pool_w[l][g], np.float32)
    kk = np.arange(128)[:, None]
    r = np.arange(128)[None, :]
    idx3 = np.minimum(r - kk + 256, 256)
    idx4 = r - kk + 128
    rb = np.asarray(rel_bias, np.float32)
    rbt = np.empty((NL, 128, 8, 2, 128), np.float32)
    rbt[:, :, :, 0, :] = rb[:, :, idx3].transpose(0, 2, 1, 3)
    rbt[:, :, :, 1, :] = rb[:, :, idx4].transpose(0, 2, 1, 3)
    return wst, cpack, cmats, pw, np.ascontiguousarray(rbt.reshape(NL, 128, 2048))


_NC_CACHE = {}


def kernel(x, norm1_g, w_in, q_norm_g, k_norm_g, rel_bias, conv_w, pool_w, pool_scale, w_out, norm2_g,
           w_mlp1, w_mlp2):
    x = np.asarray(x, np.float32)
    B, S, _ = x.shape
    NT = S // TT
    wst, cpack, cmats, pw, rbt = prep_weights(norm1_g, w_in, q_norm_g, k_norm_g, rel_bias, conv_w, pool_w,
                                              pool_scale, w_out, norm2_g, w_mlp1, w_mlp2)
    key = (NT, NL)
    if key not in _NC_CACHE:
        _NC_CACHE[key] = build_nc(NT, NL)
    nc = _NC_CACHE[key]
    in_maps = []
    for b in range(B):
        xT = np.ascontiguousarray(x[b].T).reshape(8, 128, S)
        in_maps.append({"xT": xT, "wst": wst, "cpack": cpack, "cmats": cmats, "pw": pw, "rbt": rbt})
    res = run_bass_kernel_spmd(nc, in_maps, core_ids=list(range(B)))
    out = np.empty((B, S, D), np.float32)
    for b in range(B):
        out[b] = np.asarray(res.results[b]["yT"], np.float32).reshape(D, S).T
    return out
```

```python
import os
import numpy as np
from contextlib import ExitStack
import concourse.bass as bass
import concourse.mybir as mybir
from concourse.bass_utils import run_bass_kernel_spmd

F32 = mybir.dt.float32
BF16 = mybir.dt.bfloat16
AF = mybir.ActivationFunctionType
ALU = mybir.AluOpType

D = 1024
SEQ = 4096
NL = 4
TT = 1024
NSLOT = 23
RING = 3
ENGS = ["pe", "act", "dve", "pool", "sp"]
CP_L = 26
CP_INVW = NL * CP_L
CP_INVC = CP_INVW + 2
CP_CB = CP_INVC + 32
NCOL = CP_CB + NL * 8


class Op:
    __slots__ = ("eng", "fn", "deps", "dma", "signal", "seq", "waits")

    def __init__(self, eng, fn, deps, dma):
        self.eng = eng; self.fn = fn; self.deps = deps; self.dma = dma
        self.signal = False; self.seq = 0; self.waits = ()


class Prog:
    def __init__(self):
        self.ops = []
        self.recs = {}
        self.tags = [] if os.environ.get("KTAGS") else None
        self.tag = "pro"

    @staticmethod
    def _box(ap):
        t = ap.tensor
        row = 1
        for d in list(t.shape)[1:]:
            row *= int(d)
        e = 4 if t.dtype == F32 else 2
        off = int(ap.offset)
        p0 = off // row
        f0 = off % row
        dims = ap.ap
        pc = int(dims[0][1])
        ext = 0
        for st, cn in dims[1:]:
            ext += (int(cn) - 1) * int(st)
        return (t.name, p0, p0 + pc, f0 * e, (f0 + ext + 1) * e)

    def add(self, eng, fn, reads=(), writes=(), dma=None):
        oid = len(self.ops)
        ops = self.ops
        rb = [self._box(a) for a in reads]
        wb = [self._box(a) for a in writes]
        need = set()
        for (n, p0, p1, b0, b1) in rb:
            for r in self.recs.get(n, ()):
                if r[5] and r[0] < p1 and p0 < r[1] and r[2] < b1 and b0 < r[3]:
                    need.add((r[4], 0))
        for (n, p0, p1, b0, b1) in wb:
            for r in self.recs.get(n, ()):
                if r[0] < p1 and p0 < r[1] and r[2] < b1 and b0 < r[3]:
                    need.add((r[4], 1))
        deps = set()
        for pid, kind in need:
            Pp = ops[pid]
            if Pp.dma is None and Pp.eng == eng:
                if eng == "pe":
                    continue
                if dma is None and kind == 1:
                    continue
            deps.add(pid)
        ops.append(Op(eng, fn, deps, dma))
        if self.tags is not None:
            self.tags.append((eng, self.tag))
        for (n, p0, p1, b0, b1) in wb:
            lst = self.recs.setdefault(n, [])
            lst[:] = [r for r in lst if not (p0 <= r[0] and r[1] <= p1 and b0 <= r[2] and r[3] <= b1)]
            lst.append([p0, p1, b0, b1, oid, True])
        for (n, p0, p1, b0, b1) in rb:
            lst = self.recs.setdefault(n, [])
            for r in lst:
                if (not r[5]) and r[0] == p0 and r[1] == p1 and r[2] == b0 and r[3] == b1:
                    Pp = ops[r[4]]
                    if Pp.eng == eng and Pp.dma is None and dma is None:
                        r[4] = oid
                        break
            else:
                lst.append([p0, p1, b0, b1, oid, False])
        return oid

    def finalize(self):
        ops = self.ops
        for op in ops:
            for pid in op.deps:
                ops[pid].signal = True
        cnt = {e: 0 for e in ENGS}
        dcnt = {}
        for op in ops:
            if op.dma is not None:
                dcnt[op.dma] = dcnt.get(op.dma, 0) + 16
                op.seq = dcnt[op.dma]
            elif op.signal:
                cnt[op.eng] += 1
                op.seq = cnt[op.eng]
        waited = {e: {} for e in ENGS}
        for op in ops:
            w = {}
            for pid in op.deps:
                Pp = ops[pid]
                key = ("d", Pp.dma) if Pp.dma is not None else ("e", Pp.eng)
                if w.get(key, 0) < Pp.seq:
                    w[key] = Pp.seq
            wl = []
            we = waited[op.eng]
            for key, val in w.items():
                if we.get(key, 0) < val:
                    we[key] = val
                    wl.append((key, val))
            op.waits = wl
        return sorted(dcnt.keys())


def MM(out, lhsT, rhs, st, sp):
    return lambda e: e.matmul(out, lhsT=lhsT, rhs=rhs, start=st, stop=sp)


def ACT(out, in_, func, **kw):
    return lambda e: e.activation(out=out, in_=in_, func=func, **kw)


def TTO(out, in0, in1, op):
    return lambda e: e.tensor_tensor(out=out, in0=in0, in1=in1, op=op)


def TS(out, in0, s1, op0):
    return lambda e: e.tensor_scalar(out=out, in0=in0, scalar1=s1, scalar2=None, op0=op0)


def STT(out, in0, scalar, in1, op0, op1):
    return lambda e: e.scalar_tensor_tensor(out=out, in0=in0, scalar=scalar, in1=in1, op0=op0, op1=op1)


def CPY(out, in_):
    return lambda e: e.tensor_copy(out=out, in_=in_)


def MSET(ap, v):
    return lambda e: e.memset(ap, v)


def RCP(out, in_):
    return lambda e: e.reciprocal(out=out, in_=in_)


def DMA(out, in_):
    return lambda e: e.dma_start(out=out, in_=in_)


def build_nc(NT=4, NLAY=NL):
    nc = bass.Bass("TRN2", target_bir_lowering=False)
    S_ = NT * TT
    xT_d = nc.dram_tensor("xT", [8, 128, S_], F32, kind="ExternalInput").ap()
    wst_d = nc.dram_tensor("wst", [NL, NSLOT, 128, 4096], F32, kind="ExternalInput").ap()
    cpack_d = nc.dram_tensor("cpack", [128, NCOL], F32, kind="ExternalInput").ap()
    cmats_d = nc.dram_tensor("cmats", [128, 3, 128], F32, kind="ExternalInput").ap()
    pw_d = nc.dram_tensor("pw", [128, NL * 2, 128], F32, kind="ExternalInput").ap()
    rbt_d = nc.dram_tensor("rbt", [NL, 128, 2048], F32, kind="ExternalInput").ap()
    y_d = nc.dram_tensor("yT", [8, 128, S_], F32, kind="ExternalOutput").ap()

    P = Prog()
    es = ExitStack()

    def sb(name, shape, dt):
        return es.enter_context(nc.sbuf_tensor(name, shape, dt))

    x_sb = sb("x_sb", [128, 8, TT], F32)
    h_sb = sb("h_sb", [128, 8, TT], BF16)
    U = sb("U", [128, 32768], BF16)
    KH = sb("KH", [128, NL, 4, 512], BF16)
    VH = sb("VH", [128, NL, 2080], BF16)
    WR = sb("WR", [128, RING, 4096], BF16)
    EB = sb("EB", [128, NL, 8, 256], BF16)
    rstd = sb("rstd", [128, 2, 512], F32)
    sqb = sb("sqb", [128, 2, 512], BF16)
    rl = sb("rl", [128, 2, 512], F32)
    PT = sb("PT", [128, 3, 640], BF16)
    ao = sb("ao", [128, 2, 512], BF16)
    rc = sb("rc", [128, 2, 8], F32)
    cpk = sb("cpk", [128, NCOL], F32)
    cm = sb("cm", [128, 3, 128], BF16)
    PW = sb("PW", [128, NL * 2, 128], BF16)
    zh = sb("zh", [128, NL, 2, 2], F32)
    uh = sb("uh", [128, NL, 2, 16], F32)
    epsc = sb("epsc", [128, 1], F32)
    ps = es.enter_context(nc.psum_tensor("ps", [128, 8, 512], F32))

    q_v = U[:, 0:4096].rearrange("p (j t) -> p j t", j=4)
    k_v = U[:, 4096:8192].rearrange("p (j t) -> p j t", j=4)
    v_flat = U[:, 8192:8192 + 4160]
    v_v = v_flat.rearrange("p (b h d) -> p b h d", b=8, h=8)
    mix_v = U[:, 12352:12352 + 8192].rearrange("p (j t) -> p j t", j=8)
    xsq = U[:, 20544:20544 + 4096].rearrange("p (c t) -> p c t", c=8)
    gcz = U[:, 20544:20544 + 2056].bitcast(F32).rearrange("p (c t) -> p c t", c=2)
    gb_v = U[:, 22600:22600 + 2048].bitcast(F32).rearrange("p (c t) -> p c t", c=2)
    acc1 = U[:, 24648:24648 + 1024].bitcast(F32)
    mbf2 = U[:, 25672:25672 + 1024].rearrange("p (c t) -> p c t", c=2)
    ubuf = U[:, 26696:26696 + 2112].bitcast(F32).rearrange("p (c t) -> p c t", c=2)
    sA = U[:, 28808:28808 + 1056].bitcast(F32)
    sB = U[:, 29864:29864 + 1056].bitcast(F32)
    mbf_a = U[:, 30920:30920 + 1024].rearrange("p (c t) -> p c t", c=2)
    f_v = U[:, :].rearrange("p (s f t) -> p s f t", s=2, f=32)

    ones_m = cm[:, 0, :]
    blk_m = cm[:, 1, :]
    idn_m = cm[:, 2, :]
    eps_ap = epsc[:, 0:1]

    def col(i):
        return cpk[:, i:i + 1]

    MUL = ALU.mult
    ADD = ALU.add
    SUB = ALU.subtract

    st = {"big": 0, "aux": 0, "pool": 0, "rstd": 0, "sq": 0, "rl": 0, "use": 0, "issued": 0}
    n_uses = NT * NLAY * NSLOT

    def bigP():
        b = st["big"] % 4
        st["big"] += 1
        return ps[:, b, :]

    def auxP():
        b = 4 + st["aux"] % 2
        st["aux"] += 1
        return ps[:, b, :]

    def poolP():
        b = 6 + st["pool"] % 2
        st["pool"] += 1
        return ps[:, b, :]

    def rot(name, tensor):
        i = st[name] % 2
        st[name] += 1
        return tensor[:, i, :]

    def issue_w():
        u = st["issued"]
        if u >= n_uses:
            return
        st["issued"] += 1
        si = u % NSLOT
        l = (u // NSLOT) % NLAY
        slot = u % RING
        P.add("pool", DMA(WR[:, slot, :], wst_d[l, si]), writes=[WR[:, slot, :]], dma="w%d" % slot)

    def cur_w(k=0):
        return WR[:, (st["use"] + k) % RING, :]

    def release_w(n=1):
        for _ in range(n):
            st["use"] += 1
            issue_w()

    P.add("sp", DMA(cpk[:, :], cpack_d), writes=[cpk[:, :]], dma="c0")
    P.add("pool", DMA(cm[:, :, :], cmats_d), writes=[cm[:, :, :]], dma="c1")
    P.add("pool", DMA(PW[:, :, :], pw_d), writes=[PW[:, :, :]], dma="c2")
    for _ in range(RING):
        issue_w()
    P.add("dve", MSET(epsc[:, :], 1e-6), writes=[epsc[:, :]])
    P.add("dve", MSET(zh[:, :, :, :], 0.0), writes=[zh[:, :, :, :]])
    P.add("dve", MSET(uh[:, :, :, :], 0.0), writes=[uh[:, :, :, :]])
    for l in range(NLAY):
        stg = U[:, l * 4096:(l + 1) * 4096].bitcast(F32)
        stg3 = stg.rearrange("p (h e) -> p h e", h=8)
        P.add("sp", DMA(stg, rbt_d[l]), writes=[stg], dma="r%d" % l)
        cb = cpk[:, CP_CB + l * 8:CP_CB + l * 8 + 8]
        P.add("dve", TTO(stg3, stg3, cb.unsqueeze(2).to_broadcast([128, 8, 256]), SUB),
              reads=[stg, cb], writes=[stg])
        ebl = EB[:, l].rearrange("p h e -> p (h e)")
        P.add("act", ACT(ebl, stg, AF.Exp), reads=[stg], writes=[ebl])
        ebm = EB[64:128, l].rearrange("p h (j r) -> p h j r", j=2)[:, :, 1, 0:64]
        P.add("dve", MSET(ebm, 0.0), writes=[ebm])

    deferred = []

    def flush_deferred():
        while deferred:
            deferred.pop(0)()

    def phase_norm(gbase):
        for s in range(2):
            sl = slice(s * 512, (s + 1) * 512)
            for c in range(8):
                xs = x_sb[:, c, sl]
                if True:
                    P.add("act", ACT(xsq[:, c, :], xs, AF.Square), reads=[xs], writes=[xsq[:, c, :]])
                else:
                    P.add("dve", TTO(xsq[:, c, :], xs, xs, MUL), reads=[xs], writes=[xsq[:, c, :]])
            a = auxP()
            for c in range(8):
                P.add("pe", MM(a, ones_m, xsq[:, c, :], c == 0, c == 7), reads=[ones_m, xsq[:, c, :]], writes=[a])
            r = rot("rstd", rstd)
            P.add("act", ACT(r, a, AF.Ln, bias=eps_ap, scale=1.0), reads=[a, eps_ap], writes=[r])
            P.add("act", ACT(r, r, AF.Exp, scale=-0.5), reads=[r], writes=[r])
            for c in range(8):
                g = col(gbase + c)
                P.add("dve", STT(h_sb[:, c, sl], x_sb[:, c, sl], g, r, MUL, MUL),
                      reads=[x_sb[:, c, sl], g, r], writes=[h_sb[:, c, sl]])

    def mm_group(Pb, W, c0, rhs_t, sl):
        for c in range(8):
            P.add("pe", MM(Pb, W[:, c, c0:c0 + 128], rhs_t[:, c, sl], c == 0, c == 7),
                  reads=[W[:, c, c0:c0 + 128], rhs_t[:, c, sl]], writes=[Pb])

    def phase_qk(dst, gcol):
        W = cur_w().rearrange("p (c e) -> p c e", c=8)
        pend = []

        def post(j, s, Pb):
            sl = slice(s * 512, (s + 1) * 512)
            sq = rot("sq", sqb)
            P.add("act", ACT(sq, Pb, AF.Square), reads=[Pb], writes=[sq])
            a = auxP()
            P.add("pe", MM(a, blk_m, sq, True, True), reads=[blk_m, sq], writes=[a])
            r = rot("rstd", rstd)
            P.add("act", ACT(r, a, AF.Ln, bias=eps_ap, scale=1.0), reads=[a, eps_ap], writes=[r])
            P.add("act", ACT(r, r, AF.Exp, scale=-0.5), reads=[r], writes=[r])
            P.add("dve", STT(dst[:, j, sl], Pb, gcol, r, MUL, MUL), reads=[Pb, gcol, r], writes=[dst[:, j, sl]])

        for s in range(2):
            for j in range(4):
                Pb = bigP()
                mm_group(Pb, W, j * 128, h_sb, slice(s * 512, (s + 1) * 512))
                if pend:
                    post(*pend.pop())
                pend.append((j, s, Pb))
        post(*pend.pop())
        release_w()

    def phase_v():
        W = cur_w().rearrange("p (c e) -> p c e", c=8)
        onesc = v_v[:, :, :, 64:65]
        P.add("dve", MSET(onesc, 1.0), writes=[onesc])
        for b in range(8):
            Pb = bigP()
            for c in range(8):
                lh = h_sb[:, c, b * 128:(b + 1) * 128]
                P.add("pe", MM(Pb, lh, W[:, c, :], c == 0, c == 7), reads=[lh, W[:, c, :]], writes=[Pb])
            dst = v_v[:, b, :, 0:64]
            src = Pb.rearrange("p (h d) -> p h d", h=8)
            if b % 2 == 0:
                P.add("dve", CPY(dst, src), reads=[Pb], writes=[dst])
            else:
                P.add("act", ACT(dst, src, AF.Copy), reads=[Pb], writes=[dst])
        release_w()

    def phase_convpool(l, tile):
        W3 = cur_w(0).rearrange("p (c e) -> p c e", c=8)
        W4 = cur_w(1).rearrange("p (c e) -> p c e", c=8)
        cb0 = l * CP_L

        def cw(cc, i):
            return col(cb0 + 18 + cc * 3 + i)

        for s in range(2):
            sl = slice(s * 512, (s + 1) * 512)
            mbf = mbf_a if s == 0 else mbf2
            P.add("act", ACT(gcz[:, :, 0:2], zh[:, l], AF.Copy), reads=[zh[:, l]], writes=[gcz[:, :, 0:2]])
            P.add("act", ACT(ubuf[:, :, 0:16], uh[:, l], AF.Copy), reads=[uh[:, l]], writes=[ubuf[:, :, 0:16]])
            for cc in range(2):
                Pb = bigP()
                mm_group(Pb, W3, cc * 128, h_sb, sl)
                P.add("act", ACT(gb_v[:, cc, :], Pb, AF.Copy), reads=[Pb], writes=[gb_v[:, cc, :]])
            for cc in range(2):
                Pb = bigP()
                mm_group(Pb, W3, 256 + cc * 128, h_sb, sl)
                P.add("act", ACT(gcz[:, cc, 2:514], Pb, AF.Copy), reads=[Pb], writes=[gcz[:, cc, 2:514]])
            for cc in range(2):
                Pb = bigP()
                mm_group(Pb, W4, cc * 128, h_sb, sl)
                z = gcz[:, cc, 2:514]
                P.add("dve", TTO(z, Pb, z, MUL), reads=[Pb, z], writes=[z])
            for cc in range(2):
                Pb = bigP()
                mm_group(Pb, W4, 256 + cc * 128, h_sb, sl)
                P.add("act", ACT(ubuf[:, cc, 16:528], Pb, AF.Copy), reads=[Pb], writes=[ubuf[:, cc, 16:528]])
            P.add("act", ACT(zh[:, l], gcz[:, :, 512:514], AF.Copy), reads=[gcz[:, :, 512:514]], writes=[zh[:, l]])
            P.add("act", ACT(uh[:, l], ubuf[:, :, 512:528], AF.Copy), reads=[ubuf[:, :, 512:528]], writes=[uh[:, l]])
            for cc in range(2):
                P.add("act", ACT(acc1, gcz[:, cc, 2:514], AF.Copy, scale=cw(cc, 2)),
                      reads=[gcz[:, cc, 2:514], cw(cc, 2)], writes=[acc1])
                P.add("dve", STT(acc1, gcz[:, cc, 1:513], cw(cc, 1), acc1, MUL, ADD),
                      reads=[gcz[:, cc, 1:513], cw(cc, 1), acc1], writes=[acc1])
                P.add("dve", STT(acc1, gcz[:, cc, 0:512], cw(cc, 0), acc1, MUL, ADD),
                      reads=[gcz[:, cc, 0:512], cw(cc, 0), acc1], writes=[acc1])
                P.add("dve", TTO(mix_v[:, 4 + cc, sl], acc1, gb_v[:, cc, :], MUL),
                      reads=[acc1, gb_v[:, cc, :]], writes=[mix_v[:, 4 + cc, sl]])
            for cc in range(2):
                Uc = ubuf[:, cc, :]
                P.add("dve", TTO(sA[:, 1:528], Uc[:, 1:528], Uc[:, 0:527], ADD),
                      reads=[Uc], writes=[sA[:, 1:528]])
                P.add("dve", TTO(sB[:, 3:528], sA[:, 3:528], sA[:, 1:526], ADD),
                      reads=[sA[:, 1:528]], writes=[sB[:, 3:528]])
                if cc == 1:
                    P.add("dve", TTO(sA[:, 7:528], sB[:, 7:528], sB[:, 3:524], ADD),
                          reads=[sB[:, 3:528]], writes=[sA[:, 7:528]])
                    P.add("dve", TTO(sB[:, 15:528], sA[:, 15:528], sA[:, 7:520], ADD),
                          reads=[sA[:, 7:528]], writes=[sB[:, 15:528]])
                for (pr, src) in ((slice(0, 64), sA), (slice(64, 128), sB)):
                    iw = cpk[pr, CP_INVW + cc:CP_INVW + cc + 1]
                    P.add("dve", STT(mbf[pr, cc, :], src[pr, 16:528], iw, Uc[pr, 16:528], MUL, SUB),
                          reads=[src[pr, 16:528], iw, Uc[pr, 16:528]], writes=[mbf[pr, cc, :]])
                    if tile == 0 and s == 0:
                        ic = cpk[pr, CP_INVC + cc * 16:CP_INVC + cc * 16 + 16]
                        P.add("dve", TTO(src[pr, 16:32], src[pr, 16:32], ic, MUL),
                              reads=[src[pr, 16:32], ic], writes=[src[pr, 16:32]])
                        P.add("dve", TTO(mbf[pr, cc, 0:16], src[pr, 16:32], Uc[pr, 16:32], SUB),
                              reads=[src[pr, 16:32], Uc[pr, 16:32]], writes=[mbf[pr, cc, 0:16]])

            def pool_mm(l=l, sl=sl, s=s, mbf=mbf):
                for cc in range(2):
                    pb = poolP()
                    P.add("pe", MM(pb, PW[:, l * 2 + cc, :], mbf[:, cc, :], True, True),
                          reads=[PW[:, l * 2 + cc, :], mbf[:, cc, :]], writes=[pb])
                    sc = col(cb0 + 24 + cc)
                    P.add("act", ACT(mix_v[:, 6 + cc, sl], pb, AF.Copy, scale=sc),
                          reads=[pb, sc], writes=[mix_v[:, 6 + cc, sl]])
            deferred.append(pool_mm)
        release_w(2)

    def phase_attn(l, tile):
        def S(m, h):
            jjs = [jj for jj in range(5) if (m - 4 + jj >= 0 or tile > 0)]
            lo = jjs[0] * 128
            j, hh = divmod(h, 2)
            pr = slice(hh * 64, hh * 64 + 64)
            scb = ps[:, 4 + 2 * hh:6 + 2 * hh, :].rearrange("p a b -> p (a b)")
            qs = slice(m * 128, (m + 1) * 128)
            for jj in jjs:
                kt = m - 4 + jj
                if kt >= 0:
                    lh = k_v[pr, j, kt * 128:(kt + 1) * 128]
                else:
                    lh = KH[pr, l, j, (kt + 4) * 128:(kt + 5) * 128]
                o = scb[:, jj * 128:(jj + 1) * 128]
                P.add("pe", MM(o, lh, q_v[pr, j, qs], True, True), reads=[lh, q_v[pr, j, qs]], writes=[o])
            pti = (m * 8 + h) % 3
            ptb = PT[:, pti, :]
            P.add("act", ACT(ptb[:, lo:640], scb[:, lo:640], AF.Exp, scale=0.125),
                  reads=[scb[:, lo:640]], writes=[ptb[:, lo:640]])
            P.add("dve", TTO(ptb[:, 384:640], ptb[:, 384:640], EB[:, l, h, :], MUL),
                  reads=[ptb[:, 384:640], EB[:, l, h, :]], writes=[ptb[:, 384:640]])
            if 0 in jjs:
                P.add("dve", MSET(PT[0:64, pti, 64:128], 0.0), writes=[PT[0:64, pti, 64:128]])

        def PV(m, h):
            jjs = [jj for jj in range(5) if (m - 4 + jj >= 0 or tile > 0)]
            hh = h % 2
            ob = ps[:, h // 4, (h % 4) * 65:(h % 4) * 65 + 65]
            for jj in jjs:
                kt = m - 4 + jj
                if kt >= 0:
                    rh = v_v[:, kt, h, :]
                else:
                    rh = VH[:, l, (kt + 4) * 520 + h * 65:(kt + 4) * 520 + h * 65 + 65]
                lh = PT[:, (m * 8 + h) % 3, jj * 128:(jj + 1) * 128]
                P.add("pe", MM(ob, lh, rh, jj == jjs[0], jj == 4), reads=[lh, rh], writes=[ob])

        def TP(m):
            ab = m % 2
            qs = slice(m * 128, (m + 1) * 128)
            tb = ps[:, 2 + ab, :]
            for j in range(4):
                lh = ao[:, ab, j * 128:(j + 1) * 128]
                P.add("pe", MM(tb[:, j * 128:(j + 1) * 128], lh, idn_m, True, True),
                      reads=[lh, idn_m], writes=[tb[:, j * 128:(j + 1) * 128]])
            P.add("act", ACT(mix_v[:, 0:4, qs], tb.rearrange("p (j t) -> p j t", j=4), AF.Copy),
                  reads=[tb], writes=[mix_v[:, 0:4, qs]])

        seq = [(m, h) for m in range(8) for h in range(8)]
        S(*seq[0])
        S(*seq[1])
        for i, (m, h) in enumerate(seq):
            ab = m % 2
            qs = slice(m * 128, (m + 1) * 128)
            if i + 2 < len(seq):
                S(*seq[i + 2])
            PV(m, h)
            if h == 0 and m == 3:
                flush_deferred()
            if h % 4 == 3:
                g = h // 4
                ob3 = ps[:, g, 0:260].rearrange("p (h d) -> p h d", h=4)
                rcv = rc[:, ab, g * 4:(g + 1) * 4]
                P.add("dve", RCP(rcv, ob3[:, :, 64]), reads=[ob3[:, :, 64]], writes=[rcv])
                aov = ao[:, ab, g * 256:(g + 1) * 256]
                P.add("dve", TTO(aov.rearrange("p (h d) -> p h d", h=4), ob3[:, :, 0:64],
                                 rcv.unsqueeze(2).to_broadcast([128, 4, 64]), MUL),
                      reads=[ps[:, g, 0:260], rcv], writes=[aov])
            if h == 1 and m >= 1:
                TP(m - 1)
        TP(7)
        P.add("act", ACT(KH[:, l], k_v[:, :, 512:1024], AF.Copy), reads=[k_v[:, :, 512:1024]], writes=[KH[:, l]])
        P.add("act", ACT(VH[:, l, :], U[:, 8192 + 2080:8192 + 4160], AF.Copy),
              reads=[U[:, 8192 + 2080:8192 + 4160]], writes=[VH[:, l, :]])

    def phase_B():
        for dh in range(2):
            W = cur_w().rearrange("p (c e) -> p c e", c=8)
            for dcl in range(4):
                dc = dh * 4 + dcl
                for s in range(2):
                    sl = slice(s * 512, (s + 1) * 512)
                    Pb = bigP()
                    mm_group(Pb, W, dcl * 128, mix_v, sl)
                    xs = x_sb[:, dc, sl]
                    P.add("dve", TTO(xs, Pb, xs, ADD), reads=[Pb, xs], writes=[xs])
            release_w()

    def phase_M1():
        for fg in range(8):
            W = cur_w().rearrange("p (c e) -> p c e", c=8)
            for s in range(2):
                for j in range(4):
                    fc = fg * 4 + j
                    sl = slice(s * 512, (s + 1) * 512)
                    Pb = bigP()
                    mm_group(Pb, W, j * 128, h_sb, sl)
                    r = rot("rl", rl)
                    P.add("act", ACT(r, Pb, AF.Relu), reads=[Pb], writes=[r])
                    P.add("dve", TTO(f_v[:, s, fc, :], r, r, MUL), reads=[r], writes=[f_v[:, s, fc, :]])
            release_w()

    def phase_M2(after_dc=None):
        for dc in range(8):
            W = cur_w().rearrange("p (f d) -> p f d", f=32)
            for s in range(2):
                sl = slice(s * 512, (s + 1) * 512)
                Pb = bigP()
                for fc in range(32):
                    P.add("pe", MM(Pb, W[:, fc, :], f_v[:, s, fc, :], fc == 0, fc == 31),
                          reads=[W[:, fc, :], f_v[:, s, fc, :]], writes=[Pb])
                xs = x_sb[:, dc, sl]
                P.add("dve", TTO(xs, Pb, xs, ADD), reads=[Pb, xs], writes=[xs])
            release_w()
            if after_dc is not None:
                after_dc(dc)

    import os
    kstop = int(os.environ.get("KSTOP", "99"))
    for tile in range(NT):
        t0 = tile * TT
        for dc in range(8):
            P.add("sp", DMA(x_sb[:, dc, :], xT_d[dc, :, t0:t0 + TT]), writes=[x_sb[:, dc, :]], dma="xi%d" % dc)

        def store_dc(dc, t0=t0):
            P.add("sp", DMA(y_d[dc, :, t0:t0 + TT], x_sb[:, dc, :]), reads=[x_sb[:, dc, :]], dma="xo%d" % dc)
        for l in range(NLAY):
            cb0 = l * CP_L
            P.tag = "N1"
            if kstop >= 1: phase_norm(cb0 + 0)
            P.tag = "Q"
            if kstop >= 2: phase_qk(q_v, col(cb0 + 16))
            P.tag = "K"
            if kstop >= 3: phase_qk(k_v, col(cb0 + 17))
            P.tag = "V"
            if kstop >= 4: phase_v()
            P.tag = "CP"
            if kstop >= 5: phase_convpool(l, tile)
            P.tag = "AT"
            if kstop >= 6: phase_attn(l, tile)
            flush_deferred()
            P.tag = "B"
            if kstop >= 7: phase_B()
            P.tag = "N2"
            if kstop >= 8: phase_norm(cb0 + 8)
            P.tag = "M1"
            if kstop >= 9: phase_M1()
            P.tag = "M2"
            if kstop >= 10: phase_M2(store_dc if l == NLAY - 1 else None)
    P.add("sp", None, writes=[x_sb[:, :, :]])

    dkeys = P.finalize()
    sems = {}
    for e in ENGS:
        sems[("e", e)] = es.enter_context(nc.semaphore("s_" + e))
    for k in dkeys:
        sems[("d", k)] = es.enter_context(nc.semaphore("d_" + k))
    by_eng = {e: [] for e in ENGS}
    for op in P.ops:
        by_eng[op.eng].append(op)

    def emitter(en):
        def f(e):
            mysem = sems[("e", en)]
            for op in by_eng[en]:
                for key, val in op.waits:
                    e.wait_ge(sems[key], val)
                if op.fn is None:
                    continue
                ins = op.fn(e)
                if op.dma is not None:
                    ins.then_inc(sems[("d", op.dma)], 16)
                elif op.signal:
                    ins.then_inc(mysem, 1)
        return f

    with nc.Block() as block:
        block.tensor(emitter("pe"))
        block.scalar(emitter("act"))
        block.vector(emitter("dve"))
        block.gpsimd(emitter("pool"))
        block.sync(emitter("sp"))
    es.close()
    if P.tags is not None:
        nc._ktags = P.tags
    return nc


def prep_weights(norm1_g, w_in, q_norm_g, k_norm_g, rel_bias, conv_w, pool_w, pool_scale, w_out, norm2_g,
                 w_mlp1, w_mlp2):
    wst = np.empty((NL, NSLOT, 128, 4096), np.float32)
    for l in range(NL):
        wi = np.asarray(w_in[l], np.float32).reshape(8, 128, 5, 512)
        wst[l, 0:5] = wi.transpose(2, 1, 0, 3).reshape(5, 128, 4096)
        wo = np.asarray(w_out[l], np.float32).reshape(8, 128, 2, 512)
        wst[l, 5:7] = wo.transpose(2, 1, 0, 3).reshape(2, 128, 4096)
        w1 = np.asarray(w_mlp1[l], np.float32).reshape(8, 128, 8, 512)
        wst[l, 7:15] = w1.transpose(2, 1, 0, 3).reshape(8, 128, 4096)
        w2 = np.asarray(w_mlp2[l], np.float32).reshape(32, 128, 8, 128)
        wst[l, 15:23] = w2.transpose(2, 1, 0, 3).reshape(8, 128, 4096)
    cpack = np.zeros((128, NCOL), np.float32)
    p = np.arange(128)
    for l in range(NL):
        b = l * CP_L
        cpack[:, b:b + 8] = np.asarray(norm1_g[l], np.float32).reshape(8, 128).T
        cpack[:, b + 8:b + 16] = np.asarray(norm2_g[l], np.float32).reshape(8, 128).T
        cpack[:, b + 16] = np.asarray(q_norm_g[l], np.float32)[p % 64]
        cpack[:, b + 17] = np.asarray(k_norm_g[l], np.float32)[p % 64]
        cw = np.asarray(conv_w[l], np.float32)
        for cc in range(2):
            for i in range(3):
                cpack[:, b + 18 + cc * 3 + i] = cw[i, cc * 128 + p]
            cpack[:, b + 24 + cc] = np.asarray(pool_scale[l], np.float32)[cc * 128 + p]
        cpack[:, CP_CB + l * 8:CP_CB + l * 8 + 8] = np.asarray(rel_bias[l], np.float32)[:, 256][None, :]
    wins = np.array([[2, 4], [8, 16]])
    for cc in range(2):
        wv = wins[cc][p // 64].astype(np.float32)
        cpack[:, CP_INVW + cc] = np.float32(1.0) / wv
        for t in range(16):
            cpack[:, CP_INVC + cc * 16 + t] = np.float32(1.0) / np.minimum(np.float32(t + 1), wv)
    cmats = np.zeros((128, 3, 128), np.float32)
    cmats[:, 0, :] = 1.0 / 1024.0
    cmats[0:64, 1, 0:64] = 1.0 / 64.0
    cmats[64:128, 1, 64:128] = 1.0 / 64.0
    cmats[:, 2, :] = np.eye(128, dtype=np.float32)
    pw = np.zeros((128, NL * 2, 128), np.float32)
    for l in range(NL):
        for cc in range(2):
            for half in range(2):
                g = 2 * cc + half
                pw[half * 64:(half + 1) * 64, l * 2 + cc, half * 64:(half + 1) * 64] = np.asarray(pool_w[l][g], np.float32)
    kk = np.arange(128)[:, None]
    r = np.arange(128)[None, :]
    idx3 = np.minimum(r - kk + 256, 256)
    idx4 = r - kk + 128
    rb = np.asarray(rel_bias, np.float32)
    rbt = np.empty((NL, 128, 8, 2, 128), np.float32)
    rbt[:, :, :, 0, :] = rb[:, :, idx3].transpose(0, 2, 1, 3)
    rbt[:, :, :, 1, :] = rb[:, :, idx4].transpose(0, 2, 1, 3)
    return wst, cpack, cmats, pw, np.ascontiguousarray(rbt.reshape(NL, 128, 2048))


_NC_CACHE = {}


def kernel(x, norm1_g, w_in, q_norm_g, k_norm_g, rel_bias, conv_w, pool_w, pool_scale, w_out, norm2_g,
           w_mlp1, w_mlp2):
    x = np.asarray(x, np.float32)
    B, S, _ = x.shape
    NT = S // TT
    wst, cpack, cmats, pw, rbt = prep_weights(norm1_g, w_in, q_norm_g, k_norm_g, rel_bias, conv_w, pool_w,
                                              pool_scale, w_out, norm2_g, w_mlp1, w_mlp2)
    key = (NT, NL)
    if key not in _NC_CACHE:
        _NC_CACHE[key] = build_nc(NT, NL)
    nc = _NC_CACHE[key]
    in_maps = []
    for b in range(B):
        xT = np.ascontiguousarray(x[b].T).reshape(8, 128, S)
        in_maps.append({"xT": xT, "wst": wst, "cpack": cpack, "cmats": cmats, "pw": pw, "rbt": rbt})
    res = run_bass_kernel_spmd(nc, in_maps, core_ids=list(range(B)))
    out = np.empty((B, S, D), np.float32)
    for b in range(B):
        out[b] = np.asarray(res.results[b]["yT"], np.float32).reshape(D, S).T
    return out
```
